# Optimizing a Trainium2 kernel written in Bass

```python
import math
import jax, jax.numpy as jnp
from jax import lax
import numpy as np

D_MODEL = 1024
BATCH = 4
SEQ = 4096
DEPTH = 4

SSD_D_INNER = 1024
SSD_HEAD_DIM = 64
SSD_HEADS = SSD_D_INNER // SSD_HEAD_DIM
SSD_GROUPS = 2
SSD_STATE = 128
SSD_CONV = 4
SSD_CHUNK = 128
SSD_CONV_DIM = SSD_D_INNER + 2 * SSD_GROUPS * SSD_STATE

ATTN_HEAD_DIM = 128
ATTN_HEADS_PER_GROUP = 4
DILATED_GROUPS = ((128, 1), (512, 4), (2048, 16))
ATTN_HEADS = ATTN_HEADS_PER_GROUP * len(DILATED_GROUPS)
ATTN_WIDTH = ATTN_HEADS * ATTN_HEAD_DIM
ATTN_OUT_WIDTH = ATTN_HEADS_PER_GROUP * ATTN_HEAD_DIM

POOL_WINDOWS = (2, 4, 8, 16)
POOL_WIDTH = 1024
POOL_GROUP_WIDTH = POOL_WIDTH // len(POOL_WINDOWS)

N_BRANCHES = 3
D_FF = 4 * D_MODEL
EPS = 1e-6
IN_SIZES = (SSD_D_INNER, SSD_CONV_DIM, SSD_HEADS, ATTN_WIDTH, ATTN_WIDTH, ATTN_WIDTH, POOL_WIDTH, N_BRANCHES * D_MODEL)
IN_WIDTH = SSD_D_INNER + SSD_CONV_DIM + SSD_HEADS + 3 * ATTN_WIDTH + POOL_WIDTH + N_BRANCHES * D_MODEL

kernel_name = "hybrid_ssd_dilattn_pool_gated"


def _rms(x, w):
    xf = x.astype(jnp.float32)
    y = xf * lax.rsqrt(jnp.mean(xf * xf, axis=-1, keepdims=True) + EPS)
    return (y * w.astype(jnp.float32)).astype(x.dtype)


def _alibi_slopes(n):
    def pow2(k):
        start = 2.0 ** (-8.0 / k)
        return [start ** (i + 1) for i in range(k)]
    if math.log2(n).is_integer():
        s = pow2(n)
    else:
        c = 2 ** math.floor(math.log2(n))
        s = pow2(c) + pow2(2 * c)[0::2][: n - c]
    return np.sort(np.asarray(s, np.float32))[::-1].copy()


def _causal_dwconv(x, w, bias):
    ch = x.shape[-1]
    y = lax.conv_general_dilated(x, w[:, None, :].astype(x.dtype), window_strides=(1,),
                                 padding=[(SSD_CONV - 1, 0)],
                                 dimension_numbers=('NWC', 'WIO', 'NWC'),
                                 feature_group_count=ch)
    return y + bias


def _ssd(xs, dt, A, B, C, D):
    b, S, H, P = xs.shape
    G, N, L = SSD_GROUPS, SSD_STATE, SSD_CHUNK
    J = H // G
    nc = S // L
    xc = xs.reshape(b, nc, L, G, J, P)
    X = xc * dt.reshape(b, nc, L, G, J)[..., None]
    a_cum = jnp.cumsum((dt * A).reshape(b, nc, L, G, J), axis=2)
    Bc = B.reshape(b, nc, L, G, N)
    Cc = C.reshape(b, nc, L, G, N)
    seg = a_cum[:, :, :, None] - a_cum[:, :, None, :]
    causal = jnp.tril(jnp.ones((L, L), bool))[:, :, None, None]
    decay = jnp.exp(jnp.where(causal, seg, -jnp.inf))
    cb = jnp.einsum('bclgn,bcsgn->bclsg', Cc, Bc)
    y_diag = jnp.einsum('bclsgj,bcsgjp->bclgjp', cb[..., None] * decay, X)
    Xd = X * jnp.exp(a_cum[:, :, -1:] - a_cum)[..., None]
    states = jnp.einsum('bclgn,bclgjp->bcgjpn', Bc, Xd)
    chunk_decay = jnp.exp(a_cum[:, :, -1])

    def step(h, inp):
        st, dec = inp
        return dec[..., None, None] * h + st, h

    h0 = jnp.zeros((b, G, J, P, N), X.dtype)
    _, prev = lax.scan(step, h0, (jnp.moveaxis(states, 1, 0), jnp.moveaxis(chunk_decay, 1, 0)))
    prev = jnp.moveaxis(prev, 0, 1)
    y_off = jnp.einsum('bclgn,bcgjpn->bclgjp', Cc, prev) * jnp.exp(a_cum)[..., None]
    y = y_diag + y_off + D.reshape(G, J)[:, :, None] * xc
    return y.reshape(b, S, H * P)


def _dilated_window_attn(q, k, v, steps, dilation, slopes):
    b, S, h, e = q.shape
    n = S // dilation
    nb = -(-n // steps)
    n_pad = nb * steps

    def to_sub(t):
        t = t.reshape(b, n, dilation, h, e).transpose(0, 3, 2, 1, 4)
        return jnp.pad(t, ((0, 0), (0, 0), (0, 0), (0, n_pad - n), (0, 0)))

    qs, ks, vs = to_sub(q), to_sub(k), to_sub(v)
    qb = qs.reshape(b, h, dilation, nb, steps, e)

    def band(t):
        tp = jnp.pad(t, ((0, 0), (0, 0), (0, 0), (steps, 0), (0, 0)))
        prev = tp[:, :, :, :n_pad].reshape(b, h, dilation, nb, steps, e)
        cur = t.reshape(b, h, dilation, nb, steps, e)
        return jnp.concatenate([prev, cur], axis=4)

    kb, vb = band(ks), band(vs)
    s = jnp.einsum('bhdnqe,bhdnke->bhdnqk', qb, kb) * (e ** -0.5)
    qi = jnp.arange(steps)[:, None]
    kj = jnp.arange(2 * steps)[None, :]
    rel = qi + steps - kj
    blk = jnp.arange(nb)[:, None, None]
    valid = (rel >= 0) & (rel <= steps) & (blk * steps + kj - steps >= 0)
    alibi = -slopes[:, None, None] * (rel * dilation).astype(jnp.float32)
    s = jnp.where(valid, s + alibi[:, None, None], -jnp.inf)
    lse = jax.nn.logsumexp(s, axis=-1)
    p = jnp.exp(s - lse[..., None])
    o = jnp.einsum('bhdnqk,bhdnke->bhdnqe', p, vb)
    o = o.reshape(b, h, dilation, n_pad, e)[:, :, :, :n].transpose(0, 3, 2, 1, 4).reshape(b, S, h, e)
    lse = lse.reshape(b, h, dilation, n_pad)[:, :, :, :n].transpose(0, 3, 2, 1).reshape(b, S, h)
    return o, lse


def _multi_scale_pool(u, w_mix):
    b, S, _ = u.shape
    ug = u.reshape(b, S, len(POOL_WINDOWS), POOL_GROUP_WIDTH)
    cs = jnp.cumsum(ug, axis=1)
    t = jnp.arange(S)
    outs = []
    for gi, w in enumerate(POOL_WINDOWS):
        csg = cs[:, :, gi]
        shifted = jnp.pad(csg, ((0, 0), (w, 0), (0, 0)))[:, :S]
        count = jnp.minimum(t + 1, w).astype(jnp.float32)[None, :, None]
        outs.append((csg - shifted) / count - ug[:, :, gi])
    pooled = jnp.stack(outs, axis=2)
    return jnp.einsum('bsgi,gio->bsgo', pooled, w_mix).reshape(b, S, POOL_WIDTH)


def _hybrid_mixer(h, w_in, conv_w, conv_b, dt_bias, a_log, d_skip, ssd_norm_w, w_ssd_out,
                  q_norm_w, k_norm_w, w_attn_out, w_pool_mix, pool_scale, w_pool_out, w_out, slopes):
    f32 = jnp.float32
    dtype = h.dtype
    b, S, _ = h.shape
    splits = np.cumsum(IN_SIZES)[:-1].tolist()
    z, xbc, dt_raw, q, k, v, u, gates = jnp.split(h @ w_in, splits, axis=-1)

    xbc = jax.nn.silu(_causal_dwconv(xbc, conv_w, conv_b)).astype(f32)
    xs, Bm, Cm = jnp.split(xbc, [SSD_D_INNER, SSD_D_INNER + SSD_GROUPS * SSD_STATE], axis=-1)
    dt = jax.nn.softplus(dt_raw.astype(f32) + dt_bias.astype(f32))
    A = -jnp.exp(a_log.astype(f32))
    y = _ssd(xs.reshape(b, S, SSD_HEADS, SSD_HEAD_DIM), dt, A,
             Bm.reshape(b, S, SSD_GROUPS, SSD_STATE), Cm.reshape(b, S, SSD_GROUPS, SSD_STATE),
             d_skip.astype(f32))
    y = (y * jax.nn.silu(z.astype(f32))).reshape(b, S, SSD_GROUPS, SSD_D_INNER // SSD_GROUPS)
    y = (y * lax.rsqrt(jnp.mean(y * y, axis=-1, keepdims=True) + EPS)).reshape(b, S, SSD_D_INNER)
    y_ssd = (y * ssd_norm_w.astype(f32)).astype(dtype) @ w_ssd_out

    qh = _rms(q.reshape(b, S, ATTN_HEADS, ATTN_HEAD_DIM).astype(f32), q_norm_w)
    kh = _rms(k.reshape(b, S, ATTN_HEADS, ATTN_HEAD_DIM).astype(f32), k_norm_w)
    vh = v.reshape(b, S, ATTN_HEADS, ATTN_HEAD_DIM).astype(f32)
    outs, lses = [], []
    for gi, (win, dil) in enumerate(DILATED_GROUPS):
        sl = slice(gi * ATTN_HEADS_PER_GROUP, (gi + 1) * ATTN_HEADS_PER_GROUP)
        o, lse = _dilated_window_attn(qh[:, :, sl], kh[:, :, sl], vh[:, :, sl], win // dil, dil, slopes[gi])
        outs.append(o)
        lses.append(lse)
    wts = jax.nn.softmax(jnp.stack(lses, axis=0), axis=0)
    o = jnp.sum(wts[..., None] * jnp.stack(outs, axis=0), axis=0).reshape(b, S, ATTN_OUT_WIDTH)
    y_attn = o.astype(dtype) @ w_attn_out

    y_pool = (_multi_scale_pool(u.astype(f32), w_pool_mix) * pool_scale).astype(dtype) @ w_pool_out

    g_ssd, g_attn, g_pool = jnp.split(jax.nn.sigmoid(gates.astype(f32)).astype(dtype), N_BRANCHES, axis=-1)
    return (g_ssd * y_ssd + g_attn * y_attn + g_pool * y_pool) @ w_out


def setup_inputs(seed: int = 0) -> dict:
    key = jax.random.key(seed)
    ks = jax.random.split(key, 24)
    n = jax.random.normal
    L, D = DEPTH, D_MODEL
    u01 = jax.random.uniform(ks[9], (L, SSD_HEADS))
    dt0 = jnp.exp(u01 * (math.log(0.1) - math.log(0.001)) + math.log(0.001))
    return {
        "x": n(ks[0], (BATCH, SEQ, D), jnp.float32),
        "c": n(ks[1], (BATCH, D), jnp.float32),
        "w_ada": n(ks[2], (L, D, 6 * D)) * D ** -0.5,
        "b_ada": n(ks[3], (L, 6 * D)) * 0.01,
        "norm1_w": 1.0 + 0.05 * n(ks[4], (L, D)),
        "norm2_w": 1.0 + 0.05 * n(ks[5], (L, D)),
        "w_in": n(ks[6], (L, D, IN_WIDTH)) * D ** -0.5,
        "conv_w": n(ks[7], (L, SSD_CONV, SSD_CONV_DIM)) * SSD_CONV ** -0.5,
        "conv_b": n(ks[8], (L, SSD_CONV_DIM)) * 0.01,
        "dt_bias": dt0 + jnp.log(-jnp.expm1(-dt0)),
        "a_log": jnp.log(jax.random.uniform(ks[10], (L, SSD_HEADS), minval=1.0, maxval=16.0)),
        "d_skip": 1.0 + 0.05 * n(ks[11], (L, SSD_HEADS)),
        "ssd_norm_w": 1.0 + 0.05 * n(ks[12], (L, SSD_D_INNER)),
        "w_ssd_out": n(ks[13], (L, SSD_D_INNER, D)) * SSD_D_INNER ** -0.5,
        "q_norm_w": 1.0 + 0.05 * n(ks[14], (L, ATTN_HEAD_DIM)),
        "k_norm_w": 1.0 + 0.05 * n(ks[15], (L, ATTN_HEAD_DIM)),
        "w_attn_out": n(ks[16], (L, ATTN_OUT_WIDTH, D)) * ATTN_OUT_WIDTH ** -0.5,
        "w_pool_mix": n(ks[17], (L, len(POOL_WINDOWS), POOL_GROUP_WIDTH, POOL_GROUP_WIDTH)) * POOL_GROUP_WIDTH ** -0.5,
        "pool_scale": 1.0 + 0.05 * n(ks[18], (L, POOL_WIDTH)),
        "w_pool_out": n(ks[19], (L, POOL_WIDTH, D)) * POOL_WIDTH ** -0.5,
        "w_out": n(ks[20], (L, D, D)) * D ** -0.5,
        "w_ff1": n(ks[21], (L, D, D_FF)) * D ** -0.5,
        "w_ff2": n(ks[22], (L, D_FF, D)) * D_FF ** -0.5,
    }


def reference(x, c, w_ada, b_ada, norm1_w, norm2_w, w_in, conv_w, conv_b, dt_bias, a_log, d_skip,
              ssd_norm_w, w_ssd_out, q_norm_w, k_norm_w, w_attn_out, w_pool_mix, pool_scale,
              w_pool_out, w_out, w_ff1, w_ff2):
    cond = jax.nn.silu(c)
    slopes = jnp.asarray(_alibi_slopes(ATTN_HEADS)).reshape(len(DILATED_GROUPS), ATTN_HEADS_PER_GROUP)
    for l in range(DEPTH):
        mod = (cond @ w_ada[l] + b_ada[l])[:, None, :]
        sh1, sc1, g1, sh2, sc2, g2 = jnp.split(mod, 6, axis=-1)
        h = _rms(x, norm1_w[l]) * (1 + sc1) + sh1
        x = x + g1 * _hybrid_mixer(h, w_in[l], conv_w[l], conv_b[l], dt_bias[l], a_log[l], d_skip[l],
                                   ssd_norm_w[l], w_ssd_out[l], q_norm_w[l], k_norm_w[l], w_attn_out[l],
                                   w_pool_mix[l], pool_scale[l], w_pool_out[l], w_out[l], slopes)
        h = _rms(x, norm2_w[l]) * (1 + sc2) + sh2
        x = x + g2 * (jnp.square(jax.nn.relu(h @ w_ff1[l])) @ w_ff2[l])
    return x
```

```python
import math
import numpy as np
import ml_dtypes
import concourse.bass as bass
import concourse.mybir as mybir
from concourse.bass_utils import run_bass_kernel_spmd

F32 = mybir.dt.float32
BF16 = mybir.dt.bfloat16
AF = mybir.ActivationFunctionType
ALU = mybir.AluOpType

L = 4
D = 1024
S = 4096
NQ = 1024
NQT = S // NQ
EPS = 1e-6
NEG = -30000.0
GROUPS = ((128, 1), (512, 4), (2048, 16))
VL = 142
ENG = ['pe', 'dve', 'act', 'pool', 'sp']
N_CORES = 4
DUMPS = set()
LAST = {}
STAGES = {'aq1', 'aq2', 'aq3', 'aq', 'av', 'au', 'g0', 'g1', 'g2', 'ada', 'n1', 'attn', 'ssd', 'pool', 'merge', 'n2', 'ffn'}


def _alibi_slopes(n):
    def pow2(k):
        start = 2.0 ** (-8.0 / k)
        return [start ** (i + 1) for i in range(k)]
    if math.log2(n).is_integer():
        s = pow2(n)
    else:
        c = 2 ** math.floor(math.log2(n))
        s = pow2(c) + pow2(2 * c)[0::2][: n - c]
    return np.sort(np.asarray(s, np.float32))[::-1].copy()


class Prog:
    def __init__(self, dry):
        self.dry = dry
        self.ops = []
        self.lastw = {}
        self.rd_eng = {}
        self.rd_dma = {}
        self.bank = 0

    def op(self, eng, fn, reads=(), writes=(), dsem=None):
        if self.dry:
            return
        reads = list(reads)
        writes = list(writes)
        if dsem is not None:
            writes.append(('sem', dsem))
        strong = set()
        war = set()
        for k in reads:
            w = self.lastw.get(k)
            if w is not None:
                strong.add(w)
            if k[0] == 'ps':
                for r in self.rd_eng.get(k, {}).values():
                    war.add(r)
        for k in writes:
            w = self.lastw.get(k)
            if w is not None:
                strong.add(w)
            for r in self.rd_eng.get(k, {}).values():
                war.add(r)
            for r in self.rd_dma.get(k, ()):
                war.add(r)
        i = len(self.ops)
        self.ops.append((eng, fn, strong, war, dsem))
        for k in reads:
            if dsem is None:
                self.rd_eng.setdefault(k, {})[eng] = i
            else:
                self.rd_dma.setdefault(k, []).append(i)
        for k in writes:
            self.lastw[k] = i
            self.rd_eng[k] = {}
            self.rd_dma[k] = []

    def next_bank(self):
        b = self.bank
        self.bank = (self.bank + 1) % 8
        return b

    def emit(self, nc, stack):
        ops = self.ops
        n = len(ops)
        need = [None] * n
        signal = set()
        for i, (eng, fn, strong, war, dsem) in enumerate(ops):
            deps = set()
            for d in strong:
                de, _, _, _, dd = ops[d]
                if dd is None and dsem is None and de == eng and eng == 'pe':
                    continue
                deps.add(d)
            for d in war:
                de, _, _, _, dd = ops[d]
                if dd is None and dsem is None and de == eng:
                    continue
                deps.add(d)
            need[i] = deps
            for d in deps:
                if ops[d][4] is None:
                    signal.add(d)
        cnt = {e: 0 for e in ENG}
        dcnt = {}
        ticket = {}
        semnames = set('E_' + e for e in ENG)
        for i, o in enumerate(ops):
            if o[4] is not None:
                dcnt[o[4]] = dcnt.get(o[4], 0) + 16
                ticket[i] = (o[4], dcnt[o[4]])
                semnames.add(o[4])
            elif i in signal:
                cnt[o[0]] += 1
                ticket[i] = ('E_' + o[0], cnt[o[0]])
        sems = {s: stack.enter_context(nc.semaphore(s)) for s in sorted(semnames)}
        per = {e: [i for i in range(n) if ops[i][0] == e] for e in ENG}

        def run(ename, eobj):
            seen = {}
            for i in per[ename]:
                waits = {}
                for d in need[i]:
                    s, v = ticket[d]
                    if waits.get(s, 0) < v:
                        waits[s] = v
                for s, v in waits.items():
                    if seen.get(s, 0) >= v:
                        continue
                    eobj.wait_ge(sems[s], v)
                    seen[s] = v
                ins = ops[i][1](eobj)
                if i in ticket and ins is not None:
                    ins.then_inc(sems[ticket[i][0]], 16 if ops[i][4] is not None else 1)

        with nc.Block() as block:
            @block.tensor
            def _(e):
                run('pe', e)

            @block.vector
            def _(e):
                run('dve', e)

            @block.scalar
            def _(e):
                run('act', e)

            @block.gpsimd
            def _(e):
                run('pool', e)

            @block.sync
            def _(e):
                run('sp', e)
        return cnt


NSLOT = 5
PF = 3
ARENA = 81920
GR = 1024


class Ctx:
    pass


def build_program(reqs_in):
    dry = reqs_in is None
    nc = bass.Bass("TRN2", target_bir_lowering=False)
    P = Prog(dry)
    from contextlib import ExitStack
    stack = ExitStack()

    def dram_in(name, shape, dt=F32):
        return nc.dram_tensor(name, list(shape), dt, kind="ExternalInput").ap()

    x_d = dram_in("x", [S, D])
    wst_d = dram_in("wst", [L * 144, 128, 1024])
    wao_d = dram_in("wao", [L * 8, 128, 512])
    wpm_d = dram_in("wpm", [L * 8, 128, 256])
    wf2_d = dram_in("wf2", [L * 16, 128, 2048])
    wdt_d = dram_in("wdt", [L, 128, 128])
    wada_d = dram_in("wada", [L * 48, 128, 1024])
    vecs_d = dram_in("vecs", [128, L * VL + 8])
    rows_d = dram_in("rows", [128, L * 48])
    cst_d = dram_in("cst", [128, 2624])
    ab_d = dram_in("abias", [128, 3584])
    y_d = nc.dram_tensor("y", [S, D], F32, kind="ExternalOutput").ap()
    kc_d = nc.dram_tensor("kcache", [L * 12, 128, S], BF16).ap()
    vc_d = nc.dram_tensor("vcache", [L * 12, S, 128], BF16).ap()

    def sb(name, n, dt):
        return stack.enter_context(nc.sbuf_tensor("s_" + name, [128, n], dt))

    xT = sb("xT", 8 * NQ, F32)
    hT = sb("hT", 8 * NQ, BF16)
    oT = sb("oT", 4 * NQ, BF16)
    wsl = [sb("wsl%d" % i, 2048, BF16) for i in range(NSLOT)]
    cst = sb("cst", 2624, F32)
    abias = sb("abias", 3584, F32)
    vecs = sb("vecs", L * VL + 8, F32)
    rows = sb("rows", L * 48, F32)
    modv = sb("modv", L * 48, F32)
    sv = sb("sv", L * 16 + 32, F32)
    Sst = sb("Sst", L * 1024, F32)
    chalo = sb("chalo", L * 36, F32)
    phalo = sb("phalo", L * 120, F32)
    cbf = sb("cbf", 384, BF16)
    cond2 = sb("cond2", 16, F32)
    wdt_sb = sb("wdtsb", L * 128, BF16)
    arena = sb("arena", ARENA // 2, BF16)
    PS = [stack.enter_context(nc.psum_tensor("ps%d" % i, [128, 512], F32)) for i in range(8)]

    x3 = xT[:, :].rearrange("p (c t) -> p c t", t=NQ)
    h3 = hT[:, :].rearrange("p (c t) -> p c t", t=NQ)
    o3 = oT[:, :].rearrange("p (c t) -> p c t", t=NQ)
    ident = cst[:, 0:128]
    tri = cst[:, 128:256]
    negm = cst[:, 256:384]
    ones = cst[:, 384:512]
    esel = cst[:, 512:2560]
    invc = cst[:, 2560:2624]
    ident_bf = cbf[:, 0:128]
    ones_bf = cbf[:, 128:256]
    tri_bf = cbf[:, 256:384]

    def AR(off, n, dt):
        esz = 4 if dt == F32 else 2
        assert off % 4 == 0 and off + n * esz <= ARENA, (off, n)
        ap = arena[:, off // 2: off // 2 + n * esz // 2]
        if dt == F32:
            ap = ap.bitcast(F32)
        keys = [('ar', g) for g in range(off // GR, (off + n * esz - 1) // GR + 1)]
        return ap, keys

    def psk(b):
        return [('ps', b)]

    def mm(out, okeys, lhsT, lkeys, rhs, rkeys, start, stop):
        P.op('pe', lambda e: e.matmul(out, lhsT, rhs, start=start, stop=stop),
             list(lkeys) + list(rkeys), okeys)

    def tr(out, okeys, in_, ikeys, idn):
        P.op('pe', lambda e: e.transpose(out, in_, idn), list(ikeys) + ['const'], okeys)

    def act(out, okeys, in_, ikeys, func, bias=None, scale=None, extra=()):
        kw = {}
        if bias is not None:
            kw['bias'] = bias
        if scale is not None:
            kw['scale'] = scale
        P.op('act', lambda e: e.activation(out, in_, func, **kw), list(ikeys) + list(extra), okeys)

    def tt(out, okeys, a, akeys, b, bkeys, op, eng='dve'):
        P.op(eng, lambda e: e.tensor_tensor(out, a, b, op), list(akeys) + list(bkeys), okeys)

    def ts(out, okeys, a, akeys, s1, s2, op0, op1=None, extra=(), eng='dve'):
        if op1 is None:
            P.op(eng, lambda e: e.tensor_scalar(out, a, s1, None, op0), list(akeys) + list(extra), okeys)
        else:
            P.op(eng, lambda e: e.tensor_scalar(out, a, s1, s2, op0, op1), list(akeys) + list(extra), okeys)

    def stt(out, okeys, a, akeys, sc, b, bkeys, op0, op1, extra=()):
        P.op('dve', lambda e: e.scalar_tensor_tensor(out, a, sc, b, op0, op1),
             list(akeys) + list(bkeys) + list(extra), okeys)

    def cp(out, okeys, in_, ikeys, eng='dve'):
        P.op(eng, lambda e: e.tensor_copy(out, in_), ikeys, okeys)

    def dma(eng, out, okeys, in_, ikeys, sem, **kw):
        P.op(eng, lambda e: e.dma_start(out=out, in_=in_, **kw), ikeys, okeys, dsem=sem)

    def dump(name, ap, keys, n):
        if name not in DUMPS:
            return
        dt_ = nc.dram_tensor("dbg_" + name, [128, n], F32, kind="ExternalOutput").ap()
        dma('pool', dt_[:, :], [('dbg', name)], ap, keys, 'dbg', max_dma_last_dim=4096)

    reqs_out = []
    wstate = {'issued': 0, 'next': 0}

    wdram = {'wst': wst_d, 'wao': wao_d, 'wpm': wpm_d, 'wf2': wf2_d}

    def wissue(i):
        name, idx, n = reqs_in[i]
        src = wdram[name][idx]
        slot = i % NSLOT
        dma('pool', wsl[slot][:, 0:n], [('wsl', slot)], src, ['wdram'], 'ws%d' % slot,
            max_dma_last_dim=4096)

    def wnext(name, idx, n):
        if dry:
            reqs_out.append((name, idx, n))
            return wsl[0][:, 0:n], [('wsl', 0)]
        i = wstate['next']
        wstate['next'] += 1
        while wstate['issued'] < min(len(reqs_in), i + PF + 1):
            wissue(wstate['issued'])
            wstate['issued'] += 1
        slot = i % NSLOT
        return wsl[slot][:, 0:n], [('wsl', slot)]

    dma('sp', cst[:, :], ['const'], cst_d[:, :], [], 'cld')
    dma('sp', abias[:, :], ['const'], ab_d[:, :], [], 'cld')
    dma('sp', vecs[:, :], ['const'], vecs_d[:, :], [], 'cld')
    dma('sp', rows[:, :], ['const'], rows_d[:, :], [], 'cld')
    for l_ in range(L):
        dma('pool', wdt_sb[:, l_ * 128:(l_ + 1) * 128], ['const'], wdt_d[l_], [], 'cld2')
    cp(ident_bf, ['const'], ident, ['const'])
    cp(ones_bf, ['const'], ones, ['const'])
    cp(tri_bf, ['const'], tri, ['const'])
    P.op('dve', lambda e: e.memset(Sst[:, :], 0.0), [], ['Sst'])
    P.op('dve', lambda e: e.memset(chalo[:, :], 0.0), [], ['chalo'])
    P.op('dve', lambda e: e.memset(phalo[:, :], 0.0), [], ['phalo'])

    def vcol(l, off, n=1):
        return vecs[:, l * VL + off: l * VL + off + n]

    V_N1, V_N2, V_BADA, V_CW, V_CB, V_SNW, V_PS, V_QW, V_KW = 0, 8, 16, 64, 112, 124, 132, 140, 141

    cvec = vecs[:, L * VL: L * VL + 8]
    c2v = cond2[:, :].rearrange("p (k t) -> p k t", t=2)
    act(c2v[:, :, 0], ['cond'], cvec, ['const'], AF.Silu)
    act(c2v[:, :, 1], ['cond'], cvec, ['const'], AF.Silu)
    for l in range(L if 'ada' in STAGES else 0):
        b = P.next_bank()
        for fb in range(48):
            buf, bufk = AR((fb % 2) * 4096, 1024, F32)
            dma('sp', buf, bufk, wada_d[l * 48 + fb], ['wdram'], 'wf%d' % (fb % 2))
            w3 = buf.rearrange("p (k n) -> p k n", n=128)
            for kc in range(8):
                mm(PS[b][:, 2 * fb: 2 * fb + 2], psk(b), w3[:, kc, :], bufk,
                   c2v[:, kc, :], ['cond'], kc == 0, kc == 7)
        pv = PS[b][:, 0:96].rearrange("p (f t) -> p f t", t=2)
        tt(modv[:, l * 48:(l + 1) * 48], [('modv', l)], pv[:, :, 0], psk(b), vcol(l, V_BADA, 48), ['const'], ALU.add)
        stt(sv[:, l * 16: l * 16 + 8], [('sv', l)], modv[:, l * 48 + 8: l * 48 + 16], [('modv', l)], 1.0,
            vcol(l, V_N1, 8), ['const'], ALU.add, ALU.mult)
        stt(sv[:, l * 16 + 8: l * 16 + 16], [('sv', l)], modv[:, l * 48 + 32: l * 48 + 40], [('modv', l)], 1.0,
            vcol(l, V_N2, 8), ['const'], ALU.add, ALU.mult)

    def mod(l, which, dc):
        return modv[:, l * 48 + which * 8 + dc: l * 48 + which * 8 + dc + 1]

    def rmsnorm(l, which):
        sq, sqk = AR(0, 4096, F32)
        rs, rsk = AR(16384, 512, F32)
        tmp, tmpk = AR(18432, 512, F32)
        sq3 = sq.rearrange("p (c t) -> p c t", t=512)
        for s in range(2):
            tsl = slice(s * 512, (s + 1) * 512)
            for dc in range(8):
                act(sq3[:, dc, :], sqk, x3[:, dc, tsl], [('x', s)], AF.Square)
            b = P.next_bank()
            for dc in range(8):
                mm(PS[b][:, :], psk(b), ones, ['const'], sq3[:, dc, :], sqk, dc == 0, dc == 7)
            act(rs, rsk, PS[b][:, :], psk(b), AF.Ln, bias=EPS, scale=1.0 / D)
            act(rs, rsk, rs, rsk, AF.Exp, scale=-0.5)
            for dc in range(8):
                tt(tmp, tmpk, x3[:, dc, tsl], [('x', s)], rs, rsk, ALU.mult)
                scl = sv[:, l * 16 + which * 8 + dc: l * 16 + which * 8 + dc + 1]
                act(h3[:, dc, tsl], [('h', s)], tmp, tmpk, AF.Identity,
                    bias=mod(l, 0 if which == 0 else 3, dc), scale=scl, extra=[('sv', l), ('modv', l)])

    def wst_blk(l, idx):
        return ('wst', l * 144 + idx)

    W_XBC, W_Q, W_K, W_V, W_U, W_G, W_SO, W_PO, W_WO, W_F1, W_Z = 0, 12, 24, 36, 48, 56, 80, 88, 96, 104, 136

    slopes = _alibi_slopes(12)

    def attention(l, qt):
        acc, acck = AR(16384, 2048, F32)
        acc3 = acc.rearrange("p (a t) -> p a t", t=NQ)
        sq, sqk = AR(24576, 512, BF16)
        qraw, qrk = AR(25600, 512, F32)
        rstd, rsk = AR(27648, 512, F32)
        tmpS = [AR(31744, 512, F32), AR(34816, 512, F32)]
        PT = [AR(33792, 512, BF16), AR(36864, 512, BF16)]
        rec, reck = AR(37888, 1024, F32)
        for j in range(4):
            for g in range(3):
                if ('g%d' % g) not in STAGES:
                    continue
                d = GROUPS[g][1]
                hd = 4 * g + j
                npr = NQ // d
                nq = min(128, npr)
                upr = npr // nq
                U = d * upr
                m0 = npr * qt
                Qp, Qk = AR(0, 1024, BF16)
                Kp, Kk = AR(2048, 1024, BF16)
                Kh, Khk = AR(4096, d * 128, BF16)
                Vp, Vk = AR(8192, U * 128, BF16)
                Vh, Vhk = AR(12288, d * 128, BF16)
                Qp3 = Qp.rearrange("p (u i) -> p u i", i=nq)
                Kp3 = Kp.rearrange("p (u i) -> p u i", i=nq)
                Kh3 = Kh.rearrange("p (r i) -> p r i", i=128)
                Vp3 = Vp.rearrange("p (u e) -> p u e", e=128)
                Vh3 = Vh.rearrange("p (r e) -> p r e", e=128)
                for which, dst3, dk, widx, wv in (((0, Qp3, Qk, W_Q, V_QW), (1, Kp3, Kk, W_K, V_KW)) if 'aq' in STAGES else ()):
                    w, wk = wnext(*wst_blk(l, widx + hd), 1024)
                    w3 = w.rearrange("p (k n) -> p k n", n=128)
                    for s in range(2):
                        tsl = slice(s * 512, (s + 1) * 512)
                        b = P.next_bank()
                        for kc in range(8):
                            mm(PS[b][:, :], psk(b), w3[:, kc, :], wk, h3[:, kc, tsl], [('h', s)], kc == 0, kc == 7)
                        if 'aq1' in STAGES:
                            act(sq, sqk, PS[b][:, :], psk(b), AF.Square)
                            cp(qraw, qrk, PS[b][:, :], psk(b))
                        b2 = P.next_bank()
                        if 'aq2' in STAGES:
                            mm(PS[b2][:, :], psk(b2), ones_bf, ['const'], sq, sqk, True, True)
                            act(rstd, rsk, PS[b2][:, :], psk(b2), AF.Ln, bias=EPS, scale=1.0 / 128)
                            act(rstd, rsk, rstd, rsk, AF.Exp, scale=-0.5)
                        if 'aq3' not in STAGES:
                            continue
                        mpt = 512 // d
                        if d == 1:
                            dst = dst3[:, s * 4:(s + 1) * 4, :]
                            src_q = qraw.rearrange("p (u i) -> p u i", i=128)
                            src_r = rstd.rearrange("p (u i) -> p u i", i=128)
                        else:
                            flat = dst3.rearrange("p u i -> p (u i)").rearrange("p (r m) -> p r m", m=npr)
                            dst = flat[:, :, s * mpt:(s + 1) * mpt]
                            src_q = qraw.rearrange("p (m r) -> p r m", r=d)
                            src_r = rstd.rearrange("p (m r) -> p r m", r=d)
                        stt(dst, dk, src_q, qrk, vcol(l, wv), src_r, rsk, ALU.mult, ALU.mult, extra=['const'])
                w, wk = wnext(*wst_blk(l, W_V + hd), 1024)
                w3 = w.rearrange("p (k n) -> p k n", n=128)
                for u0 in range(0, U if 'av' in STAGES else 0, 4):
                    b = P.next_bank()
                    for uu in range(4):
                        u = u0 + uu
                        r, nb = u // upr, u % upr
                        t0 = r + d * nb * nq
                        for kc in range(8):
                            lh = h3[:, kc, t0: t0 + d * (nq - 1) + 1: d]
                            mm(PS[b][0:nq, uu * 128:(uu + 1) * 128], psk(b), lh, [('h', 0), ('h', 1)],
                               w3[:, kc, :], wk, kc == 0, kc == 7)
                    act(Vp3[0:nq, u0:u0 + 4, :], Vk,
                        PS[b][0:nq, :].rearrange("p (u e) -> p u e", e=128), psk(b), AF.Copy)
                kcv = kc_d[l * 12 + hd].rearrange("p (r m) -> p r m", m=S // d)
                vcv = vc_d[l * 12 + hd].rearrange("(r m) e -> m r e", m=S // d)
                if qt < NQT - 1:
                    dma('sp', kcv[:, :, m0:m0 + npr], [('kc', l, hd)],
                        Kp.rearrange("p (r m) -> p r m", m=npr), Qk + Kk, 'kst')
                    for nb in range(upr):
                        dma('sp', vcv[m0 + nb * nq: m0 + (nb + 1) * nq, :, :], [('vc', l, hd)],
                            Vp3[0:nq, nb:U:upr, :], Vk, 'vst')
                have_prev = qt > 0
                pm0 = max(0, m0 - 128)
                npv = m0 - pm0
                if have_prev:
                    if npv < 128:
                        P.op('dve', lambda e: e.memset(Kh, 0.0), [], Khk)
                        P.op('dve', lambda e: e.memset(Vh, 0.0), [], Vhk)
                    dma('sp', Kh3[:, :, 128 - npv:128], Khk, kcv[:, :, pm0:m0], [('kc', l, hd)], 'kld')
                    dma('sp', Vh3[128 - npv:128, :, :], Vhk, vcv[pm0:m0, :, :], [('vc', l, hd)], 'vld')
                sc = 128.0 ** -0.5
                abh = abias[:, hd * 256:(hd + 1) * 256]
                if g == 2 and qt == 1:
                    ab_prev = abias[:, 3072 + j * 128: 3072 + (j + 1) * 128]
                else:
                    ab_prev = abh[:, 0:128]
                ab_cur = abh[:, 128:256]
                for u in range(U if 'au' in STAGES else 0):
                    r, nb = u // upr, u % upr
                    prev = None
                    if nb >= 1:
                        prev = (Kp3[:, u - 1, :], Kk, Vp3[:, u - 1, :], Vk, abh[:, 0:128])
                    elif have_prev:
                        prev = (Kh3[:, r, :], Khk, Vh3[:, r, :], Vhk, ab_prev)
                    b = P.next_bank()
                    (tS, tSk), (pT, pTk) = tmpS[u % 2], PT[u % 2]
                    if prev is not None:
                        mm(PS[b][:, 0:nq], psk(b), prev[0], prev[1], Qp3[:, u, :], Qk, True, True)
                        stt(tS[:, 0:nq], tSk, PS[b][:, 0:nq], psk(b), sc, prev[4][:, 0:nq], ['const'],
                            ALU.mult, ALU.add)
                        act(pT[:, 0:nq], pTk, tS[:, 0:nq], tSk, AF.Exp)
                    mm(PS[b][0:nq, 128:128 + nq], psk(b), Kp3[:, u, :], Kk, Qp3[:, u, :], Qk, True, True)
                    stt(tS[0:nq, 128:128 + nq], tSk, PS[b][0:nq, 128:128 + nq], psk(b), sc,
                        ab_cur[0:nq, 0:nq], ['const'], ALU.mult, ALU.add)
                    act(pT[0:nq, 128:128 + nq], pTk, tS[0:nq, 128:128 + nq], tSk, AF.Exp)
                    b2 = P.next_bank()
                    if prev is not None:
                        mm(PS[b2][:, 0:nq], psk(b2), prev[2], prev[3], pT[:, 0:nq], pTk, True, False)
                    mm(PS[b2][:, 0:nq], psk(b2), Vp3[0:nq, u, :], Vk, pT[0:nq, 128:128 + nq], pTk,
                       prev is None, True)
                    if prev is not None:
                        mm(PS[b2][:, 128:128 + nq], psk(b2), ones_bf, ['const'], pT[:, 0:nq], pTk, True, False)
                    mm(PS[b2][:, 128:128 + nq], psk(b2), ones_bf[0:nq, :], ['const'], pT[0:nq, 128:128 + nq], pTk,
                       prev is None, True)
                    t0 = r + d * nb * nq
                    dsta = acc3[:, :, t0: t0 + d * (nq - 1) + 1: d]
                    srcp = PS[b2][:, 0:256].rearrange("p (a q) -> p a q", q=128)[:, :, 0:nq]
                    if g == 0:
                        cp(dsta, acck, srcp, psk(b2))
                    else:
                        tt(dsta, acck, srcp, psk(b2), dsta, acck, ALU.add)
            act(rec, reck, acc3[:, 1, :], acck, AF.Ln)
            act(rec, reck, rec, reck, AF.Exp, scale=-1.0)
            tt(o3[:, j, :], ['o'], acc3[:, 0, :], acck, rec, reck, ALU.mult)

    def ssd(l, qt, s, ynT3, ynk):
        tsl = slice(s * 512, (s + 1) * 512)
        xbc, xbck = AR(0, 12 * 512, BF16)
        xbc3 = xbc.rearrange("p (c t) -> p c t", t=512)
        raw, rawk = AR(12288, 516, F32)
        cacc, cak = AR(14592, 512, F32)
        sm, smk = AR(16896, 1536, F32)
        X, Xk = AR(23040, 1024, BF16)
        Xd, Xdk = AR(25088, 1024, BF16)
        Btm, Btk = AR(27136, 256, BF16)
        zs, zsk = AR(28160, 1024, F32)
        yv, yk = AR(32256, 1024, F32)
        yo, yok = AR(36352, 1024, F32)
        MT, MTk = AR(40448, 2048, BF16)
        acT, acTk = AR(44544, 128, F32)
        nacT, nacTk = AR(45568, 128, F32)
        cbT, cbTk = AR(46592, 256, F32)
        ynr, ynrk = AR(47616, 1024, BF16)
        Sbf, Sbfk = AR(49664, 1024, BF16)
        dec, deck = AR(51712, 512, F32)
        for c in range(12):
            w, wk = wnext(*wst_blk(l, W_XBC + c), 1024)
            w3 = w.rearrange("p (k n) -> p k n", n=128)
            b = P.next_bank()
            for kc in range(8):
                mm(PS[b][:, :], psk(b), w3[:, kc, :], wk, h3[:, kc, tsl], [('h', s)], kc == 0, kc == 7)
            hal = chalo[:, l * 36 + c * 3: l * 36 + c * 3 + 3]
            cp(raw[:, 0:3], rawk, hal, ['chalo'])
            act(raw[:, 3:515], rawk, PS[b][:, :], psk(b), AF.Copy)
            cp(hal, ['chalo'], raw[:, 512:515], rawk)
            ts(cacc, cak, raw[:, 0:512], rawk, vcol(l, V_CW + 0 * 12 + c), vcol(l, V_CB + c), ALU.mult, ALU.add,
               extra=['const'])
            for k in range(1, 4):
                stt(cacc, cak, raw[:, k:k + 512], rawk, vcol(l, V_CW + k * 12 + c), cacc, cak, ALU.mult, ALU.add,
                    extra=['const'])
            act(xbc3[:, c, :], xbck, cacc, cak, AF.Silu)
        zT, zTk = AR(53760, 8 * 512, BF16)
        zT3 = zT.rearrange("p (c t) -> p c t", t=512)
        for c in range(8):
            w, wk = wnext(*wst_blk(l, W_Z + c), 1024)
            w3 = w.rearrange("p (k n) -> p k n", n=128)
            b = P.next_bank()
            for kc in range(8):
                mm(PS[b][:, :], psk(b), w3[:, kc, :], wk, h3[:, kc, tsl], [('h', s)], kc == 0, kc == 7)
            act(zT3[:, c, :], zTk, PS[b][:, :], psk(b), AF.Silu)
        wdt3 = wdt_sb[:, l * 128:(l + 1) * 128].rearrange("p (k n) -> p k n", n=16)
        wdtk = ['const']
        S_l = Sst[:, l * 1024:(l + 1) * 1024]
        r_dtb = rows[:, l * 48: l * 48 + 16]
        r_alog = rows[:, l * 48 + 16: l * 48 + 32]
        r_dsk = rows[:, l * 48 + 32: l * 48 + 48]
        def smv(i):
            return sm[:, i * 16:(i + 1) * 16]
        for ch in range(4):
            c0 = s * 512 + ch * 128
            cl = slice(ch * 128, (ch + 1) * 128)
            first = (qt == 0 and s == 0 and ch == 0)
            b = P.next_bank()
            for kc in range(8):
                mm(PS[b][:, 0:16], psk(b), h3[:, kc, c0:c0 + 128], [('h', s)], wdt3[:, kc, :], wdtk, kc == 0, kc == 7)
            xr, ab_, e1, dtv, negA, dtA, acl, eac, wv_, acu, dcb = [smv(i) for i in range(11)]
            tt(xr, smk, PS[b][:, 0:16], psk(b), r_dtb, ['const'], ALU.add)
            stt(ab_, smk, xr, smk, -1.0, xr, smk, ALU.mult, ALU.max)
            act(e1, smk, ab_, smk, AF.Exp, scale=-1.0)
            act(e1, smk, e1, smk, AF.Ln, bias=1.0)
            stt(dtv, smk, xr, smk, 0.0, e1, smk, ALU.max, ALU.add)
            act(negA, smk, r_alog, ['const'], AF.Exp)
            stt(dtA, smk, dtv, smk, -1.0, negA, smk, ALU.mult, ALU.mult)
            b = P.next_bank()
            mm(PS[b][:, 0:16], psk(b), tri, ['const'], dtA, smk, True, True)
            mm(PS[b][:, 16:32], psk(b), ones, ['const'], dtA, smk, True, True)
            mm(PS[b][0:16, 128:256], psk(b), dtA, smk, tri, ['const'], True, True)
            cp(acu, smk, PS[b][:, 0:16], psk(b))
            cp(acl, smk, PS[b][:, 16:32], psk(b))
            cp(acT[0:16, :], acTk, PS[b][0:16, 128:256], psk(b))
            ts(nacT[0:16, :], nacTk, PS[b][0:16, 128:256], psk(b), -1.0, None, ALU.mult)
            act(eac, smk, acu, smk, AF.Exp)
            tt(wv_, smk, acl, smk, acu, smk, ALU.subtract)
            act(wv_, smk, wv_, smk, AF.Exp)
            tt(wv_, smk, wv_, smk, dtv, smk, ALU.mult)
            act(dcb, smk, acl, smk, AF.Exp)
            b = P.next_bank()
            psb = PS[b][:, :].bitcast(BF16)
            for c in range(8):
                tr(psb[:, c * 128:(c + 1) * 128], psk(b), xbc3[:, c, cl], xbck, ident_bf)
            b3 = P.next_bank()
            psb3 = PS[b3][:, :].bitcast(BF16)
            for gg in range(2):
                tr(psb3[:, gg * 128:(gg + 1) * 128], psk(b3), xbc3[:, 8 + gg, cl], xbck, ident_bf)
            cp(Btm, Btk, psb3[:, 0:256], psk(b3))
            xs3 = psb.rearrange("p (h q) -> p h q", q=64)
            X3 = X.rearrange("p (h q) -> p h q", q=64)
            Xd3 = Xd.rearrange("p (h q) -> p h q", q=64)
            yo3 = yo.rearrange("p (h q) -> p h q", q=64)
            tt(X3, Xk, xs3, psk(b), dtv.unsqueeze(2).to_broadcast([128, 16, 64]), smk, ALU.mult)
            tt(Xd3, Xdk, xs3, psk(b), wv_.unsqueeze(2).to_broadcast([128, 16, 64]), smk, ALU.mult)
            tt(yo3, yok, xs3, psk(b), r_dsk.unsqueeze(2).to_broadcast([128, 16, 64]), ['const'], ALU.mult)
            b = P.next_bank()
            for gg in range(2):
                mm(PS[b][:, gg * 128:(gg + 1) * 128], psk(b), xbc3[:, 8 + gg, cl], xbck, xbc3[:, 10 + gg, cl], xbck,
                   True, True)
            cb3 = cbT.rearrange("p (g q) -> p g q", q=128)
            tt(cb3, cbTk, PS[b][:, 0:256].rearrange("p (g q) -> p g q", q=128), psk(b),
               tri.unsqueeze(1).to_broadcast([128, 2, 128]), ['const'], ALU.mult)
            MT3 = MT.rearrange("p (h q) -> p h q", q=128)
            for h0 in range(0, 16, 4):
                b = P.next_bank()
                for hh in range(4):
                    h = h0 + hh
                    o_ = PS[b][:, hh * 128:(hh + 1) * 128]
                    es = esel[0:16, h * 128:(h + 1) * 128]
                    mm(o_, psk(b), es, ['const'], acT[0:16, :], acTk, True, False)
                    mm(o_, psk(b), nacT[0:16, :], nacTk, es, ['const'], False, False)
                    mm(o_, psk(b), ident, ['const'], negm, ['const'], False, True)
                act(dec, deck, PS[b][:, :], psk(b), AF.Exp)
                gg = h0 // 8
                tt(MT3[:, h0:h0 + 4, :], MTk, dec.rearrange("p (h q) -> p h q", q=128), deck,
                   cb3[:, gg, :].unsqueeze(1).to_broadcast([128, 4, 128]), cbTk, ALU.mult)
            by = [P.next_bank(), P.next_bank()]
            for h in range(16):
                mm(PS[by[h // 8]][:, (h % 8) * 64:(h % 8 + 1) * 64], psk(by[h // 8]), MT3[:, h, :], MTk,
                   X3[:, h, :], Xk, True, True)
            if not first:
                cp(Sbf, Sbfk, S_l, ['Sst'], eng='pool')
                bo = [P.next_bank(), P.next_bank()]
                for gg in range(2):
                    mm(PS[bo[gg]][:, :], psk(bo[gg]), xbc3[:, 10 + gg, cl], xbck, Sbf[:, gg * 512:(gg + 1) * 512],
                       Sbfk, True, True)
                yv3 = yv.rearrange("p (h q) -> p h q", q=64)
                for gg in range(2):
                    tt(yv3[:, gg * 8:(gg + 1) * 8, :], yk, PS[bo[gg]][:, :].rearrange("p (h q) -> p h q", q=64),
                       psk(bo[gg]), eac[:, gg * 8:(gg + 1) * 8].unsqueeze(2).to_broadcast([128, 8, 64]), smk,
                       ALU.mult)
                tt(yo, yok, yo, yok, yv, yk, ALU.add, eng='pool')
            for hb in range(2):
                tt(yv[:, hb * 512:(hb + 1) * 512], yk, PS[by[hb]][:, :], psk(by[hb]),
                   yo[:, hb * 512:(hb + 1) * 512], yok, ALU.add)
            bs = [P.next_bank(), P.next_bank()]
            for gg in range(2):
                mm(PS[bs[gg]][:, :], psk(bs[gg]), Btm[:, gg * 128:(gg + 1) * 128], Btk,
                   Xd[:, gg * 512:(gg + 1) * 512], Xdk, True, True)
            S3 = S_l.rearrange("p (h q) -> p h q", q=64)
            if not first:
                tt(S3, ['Sst'], S3, ['Sst'], dcb.unsqueeze(2).to_broadcast([128, 16, 64]), smk, ALU.mult)
                for gg in range(2):
                    tt(S_l[:, gg * 512:(gg + 1) * 512], ['Sst'], PS[bs[gg]][:, :], psk(bs[gg]),
                       S_l[:, gg * 512:(gg + 1) * 512], ['Sst'], ALU.add)
            else:
                for gg in range(2):
                    cp(S_l[:, gg * 512:(gg + 1) * 512], ['Sst'], PS[bs[gg]][:, :], psk(bs[gg]))
            bz = P.next_bank()
            pz = PS[bz][:, :].bitcast(BF16)
            for c in range(8):
                tr(pz[:, c * 128:(c + 1) * 128], psk(bz), zT3[:, c, cl], zTk, ident_bf)
            cp(zs, zsk, pz, psk(bz))
            tt(yv, yk, yv, yk, zs, zsk, ALU.mult)
            ssq = sm[:, 176:178]
            for gg in range(2):
                P.op('act', (lambda gg=gg: lambda e: e.activation(zs[:, gg * 512:(gg + 1) * 512],
                                                                  yv[:, gg * 512:(gg + 1) * 512], AF.Square,
                                                                  accum_out=ssq[:, gg:gg + 1]))(),
                     yk, zsk + smk)
            act(ssq, smk, ssq, smk, AF.Ln, bias=EPS, scale=1.0 / 512)
            act(ssq, smk, ssq, smk, AF.Exp, scale=-0.5)
            for gg in range(2):
                ts(ynr[:, gg * 512:(gg + 1) * 512], ynrk, yv[:, gg * 512:(gg + 1) * 512], yk, ssq[:, gg:gg + 1],
                   None, ALU.mult, extra=smk)
            b = P.next_bank()
            psb = PS[b][:, :].bitcast(BF16)
            for c in range(8):
                tr(psb[:, c * 128:(c + 1) * 128], psk(b), ynr[:, c * 128:(c + 1) * 128], ynrk, ident_bf)
            for c in range(8):
                act(ynT3[:, c, c0:c0 + 128], ynk, psb[:, c * 128:(c + 1) * 128], psk(b), AF.Identity,
                    scale=vcol(l, V_SNW + c), extra=['const'])

    def pool(l, qt, pm3, pmk):
        pooled, plk = AR(16384, 8 * NQ, BF16)
        pl3 = pooled.rearrange("p (c t) -> p c t", t=NQ)
        ubA, uak = AR(0, 1040, F32)
        ubB, ubk = AR(4160, 1040, F32)
        ubC, uck = AR(8320, 1040, F32)
        for c in range(8):
            gi = c // 2
            w_ = 2 << gi
            w, wk = wnext(*wst_blk(l, W_U + c), 1024)
            w3 = w.rearrange("p (k n) -> p k n", n=128)
            hal = phalo[:, l * 120 + c * 15: l * 120 + c * 15 + 15]
            cp(ubA[:, 0:15], uak, hal, ['phalo'])
            for s in range(2):
                b = P.next_bank()
                for kc in range(8):
                    mm(PS[b][:, :], psk(b), w3[:, kc, :], wk, h3[:, kc, s * 512:(s + 1) * 512], [('h', s)],
                       kc == 0, kc == 7)
                act(ubA[:, 15 + s * 512: 15 + (s + 1) * 512], uak, PS[b][:, :], psk(b), AF.Copy)
            cp(hal, ['phalo'], ubA[:, 1024:1039], uak)
            src, srck = ubA, uak
            bufs = [(ubB, ubk), (ubC, uck)]
            step = 1
            i = 0
            while step < w_:
                dst, dstk = bufs[i % 2]
                tt(dst[:, step:1039], dstk, src[:, step:1039], srck, src[:, 0:1039 - step], srck, ALU.add,
                   eng='pool')
                src, srck = dst, dstk
                step *= 2
                i += 1
            stt(pl3[:, c, :], plk, src[:, 15:1039], srck, 1.0 / w_, ubA[:, 15:1039], uak, ALU.mult, ALU.subtract)
            if qt == 0:
                tmp16 = sv[:, L * 16: L * 16 + 16]
                tt(tmp16, ['svtmp'], src[:, 15:31], srck, invc[:, gi * 16:(gi + 1) * 16], ['const'], ALU.mult)
                tt(pl3[:, c, 0:16], plk, tmp16, ['svtmp'], ubA[:, 15:31], uak, ALU.subtract)
        for gi in range(4):
            for oc in range(2):
                w, wk = wnext('wpm', l * 8 + gi * 2 + oc, 256)
                w3 = w.rearrange("p (k n) -> p k n", n=128)
                for s in range(2):
                    b = P.next_bank()
                    for ic in range(2):
                        mm(PS[b][:, :], psk(b), w3[:, ic, :], wk, pl3[:, gi * 2 + ic, s * 512:(s + 1) * 512], plk,
                           ic == 0, ic == 1)
                    act(pm3[:, gi * 2 + oc, s * 512:(s + 1) * 512], pmk, PS[b][:, :], psk(b), AF.Identity,
                        scale=vcol(l, V_PS + gi * 2 + oc), extra=['const'])

    def merge(l, ynT3, ynk, pm3, pmk):
        mT, mTk = AR(0, 8 * NQ, BF16)
        mT3 = mT.rearrange("p (c t) -> p c t", t=NQ)
        sig, sigk = AR(16384, 512, F32)
        m1, m1k = AR(18432, 512, F32)
        m2, m2k = AR(20480, 512, F32)
        macc, mack = AR(22528, 1024, F32)
        srcs = ((W_SO, ynT3, ynk, 8), (None, o3, ['o'], 4), (W_PO, pm3, pmk, 8))
        for dc in range(8):
            for br, (widx, src3, srck, nk) in enumerate(srcs):
                if widx is not None:
                    wy = wnext(*wst_blk(l, widx + dc), 1024)
                else:
                    wy = wnext('wao', l * 8 + dc, 512)
                wg = wnext(*wst_blk(l, W_G + br * 8 + dc), 1024)
                wy3 = wy[0].rearrange("p (k n) -> p k n", n=128)
                wg3 = wg[0].rearrange("p (k n) -> p k n", n=128)
                for s in range(2):
                    tsl = slice(s * 512, (s + 1) * 512)
                    b1 = P.next_bank()
                    for kc in range(nk):
                        mm(PS[b1][:, :], psk(b1), wy3[:, kc, :], wy[1], src3[:, kc, tsl], srck, kc == 0, kc == nk - 1)
                    b2 = P.next_bank()
                    for kc in range(8):
                        mm(PS[b2][:, :], psk(b2), wg3[:, kc, :], wg[1], h3[:, kc, tsl], [('h', s)], kc == 0, kc == 7)
                    act(sig, sigk, PS[b2][:, :], psk(b2), AF.Sigmoid)
                    if br == 0:
                        tt(macc[:, tsl], mack, PS[b1][:, :], psk(b1), sig, sigk, ALU.mult)
                    elif br == 1:
                        tt(m2, m2k, PS[b1][:, :], psk(b1), sig, sigk, ALU.mult)
                        tt(macc[:, tsl], mack, macc[:, tsl], mack, m2, m2k, ALU.add, eng='pool')
                    else:
                        tt(m2, m2k, PS[b1][:, :], psk(b1), sig, sigk, ALU.mult)
                        tt(mT3[:, dc, tsl], mTk, macc[:, tsl], mack, m2, m2k, ALU.add, eng='pool')
        if l == 0:
            dump('mT', mT, mTk, 8 * NQ)
        for dc in range(8):
            w, wk = wnext(*wst_blk(l, W_WO + dc), 1024)
            w3 = w.rearrange("p (k n) -> p k n", n=128)
            for s in range(2):
                tsl = slice(s * 512, (s + 1) * 512)
                b = P.next_bank()
                for kc in range(8):
                    mm(PS[b][:, :], psk(b), w3[:, kc, :], wk, mT3[:, kc, tsl], mTk, kc == 0, kc == 7)
                stt(x3[:, dc, tsl], [('x', s)], PS[b][:, :], psk(b), mod(l, 2, dc), x3[:, dc, tsl], [('x', s)],
                    ALU.mult, ALU.add, extra=[('modv', l)])

    def ffn(l):
        hid, hidk = AR(0, 32 * NQ, BF16)
        hid3 = hid.rearrange("p (c t) -> p c t", t=NQ)
        sqv, sqk = AR(65536, 512, F32)
        sqv2, sqk2 = AR(67584, 512, F32)
        for fc in range(32):
            w, wk = wnext(*wst_blk(l, W_F1 + fc), 1024)
            w3 = w.rearrange("p (k n) -> p k n", n=128)
            for s in range(2):
                tsl = slice(s * 512, (s + 1) * 512)
                b = P.next_bank()
                for kc in range(8):
                    mm(PS[b][:, :], psk(b), w3[:, kc, :], wk, h3[:, kc, tsl], [('h', s)], kc == 0, kc == 7)
                sq_, sk_ = (sqv, sqk) if s == 0 else (sqv2, sqk2)
                act(sq_, sk_, PS[b][:, :], psk(b), AF.Square)
                hk = [('ar', (fc * NQ * 2 + s * 1024) // GR)]
                stt(hid3[:, fc, tsl], hk, PS[b][:, :], psk(b), 0.0, sq_, sk_, ALU.is_gt, ALU.mult)
        for dc in range(8):
            ws_ = [wnext('wf2', l * 16 + dc * 2 + hh, 2048) for hh in range(2)]
            for s in range(2):
                tsl = slice(s * 512, (s + 1) * 512)
                b = P.next_bank()
                for fc in range(32):
                    w3 = ws_[fc // 16][0].rearrange("p (k n) -> p k n", n=128)
                    hk = [('ar', (fc * NQ * 2 + s * 1024) // GR)]
                    mm(PS[b][:, :], psk(b), w3[:, fc % 16, :], ws_[fc // 16][1], hid3[:, fc, tsl], hk,
                       fc == 0, fc == 31)
                stt(x3[:, dc, tsl], [('x', s)], PS[b][:, :], psk(b), mod(l, 5, dc), x3[:, dc, tsl], [('x', s)],
                    ALU.mult, ALU.add, extra=[('modv', l)])

    for qt in range(NQT):
        for c in range(8):
            st, stk = AR((c % 2) * 4096, 1024, F32)
            dma('sp', st, stk, x_d[qt * NQ + c * 128: qt * NQ + (c + 1) * 128, :], [],
                'xs%d' % (c % 2))
            for half in range(2):
                b = P.next_bank()
                for dd in range(4):
                    dc = half * 4 + dd
                    tr(PS[b][:, dd * 128:(dd + 1) * 128], psk(b), st[:, dc * 128:(dc + 1) * 128], stk,
                       ident)
                P.op('act', (lambda b=b, half=half, c=c: lambda e: e.activation(
                    x3[:, half * 4:(half + 1) * 4, c * 128:(c + 1) * 128],
                    PS[b][:, :].rearrange("p (a q) -> p a q", q=128), AF.Copy))(),
                     psk(b), [('x', c // 4)])
        for l in range(L):
            if 'n1' in STAGES:
                rmsnorm(l, 0)
            if l == 0 and qt == 0:
                dump('hT', hT[:, :], [('h', 0), ('h', 1)], 8 * NQ)
            if 'attn' in STAGES:
                attention(l, qt)
            if l == 0 and qt == 0:
                dump('oT', oT[:, :], ['o'], 4 * NQ)
            ynT, ynk = AR(ARENA - 16384, 8 * NQ, BF16)
            ynT3 = ynT.rearrange("p (c t) -> p c t", t=NQ)
            if 'ssd' in STAGES:
                for s in range(2):
                    ssd(l, qt, s, ynT3, ynk)
            pm, pmk = AR(32768, 8 * NQ, BF16)
            pm3 = pm.rearrange("p (c t) -> p c t", t=NQ)
            if l == 0 and qt == 0:
                dump('ynT', ynT, ynk, 8 * NQ)
            if 'pool' in STAGES:
                pool(l, qt, pm3, pmk)
            if l == 0 and qt == 0:
                dump('pm', pm, pmk, 8 * NQ)
            if 'merge' in STAGES:
                merge(l, ynT3, ynk, pm3, pmk)
            if 'n2' in STAGES:
                rmsnorm(l, 1)
            if 'ffn' in STAGES:
                ffn(l)
        for c in range(8):
            st, stk = AR((c % 2) * 4096, 1024, F32)
            for half in range(2):
                b = P.next_bank()
                for dd in range(4):
                    dc = half * 4 + dd
                    tr(PS[b][:, dd * 128:(dd + 1) * 128], psk(b), x3[:, dc, c * 128:(c + 1) * 128], [('x', c // 4)],
                       ident)
                act(st[:, half * 512:(half + 1) * 512], stk, PS[b][:, :], psk(b), AF.Copy)
            dma('sp', y_d[qt * NQ + c * 128: qt * NQ + (c + 1) * 128, :], [('y', qt, c)], st,
                stk, 'xs%d' % (c % 2))
    P.op('sp', lambda e: None, [('sem', 'xs0'), ('sem', 'xs1')] + ([('sem', 'dbg')] if DUMPS else []), [])

    if dry:
        stack.close()
        return reqs_out
    cnt = P.emit(nc, stack)
    stack.close()
    return nc, len(P.ops), cnt


def _blk_st(W, ncol=128):
    K, N = W.shape
    return np.ascontiguousarray(
        W.reshape(K // 128, 128, N // ncol, ncol).transpose(2, 1, 0, 3).reshape(N // ncol, 128, (K // 128) * ncol))


def _pvec(v):
    return np.ascontiguousarray(v.reshape(-1, 128).T)


def _consts():
    cst = np.zeros((128, 2624), np.float32)
    i = np.arange(128)
    cst[:, 0:128] = np.eye(128)
    cst[:, 128:256] = (i[:, None] <= i[None, :])
    cst[:, 256:384] = np.where(i[None, :] >= i[:, None], 0.0, NEG)
    cst[:, 384:512] = 1.0
    for h in range(16):
        cst[h, 512 + h * 128: 512 + (h + 1) * 128] = 1.0
    for gi in range(4):
        w = 2 << gi
        cst[:, 2560 + gi * 16: 2560 + (gi + 1) * 16] = 1.0 / np.minimum(np.arange(16) + 1, w)
    slopes = _alibi_slopes(12)
    ab = np.zeros((128, 3584), np.float32)
    k = i[:, None].astype(np.float32)
    q = i[None, :].astype(np.float32)
    for hd in range(12):
        g = hd // 4
        a = float(slopes[hd]) * GROUPS[g][1]
        ab[:, hd * 256: hd * 256 + 128] = np.where(k >= q, -a * (q + 128 - k), NEG)
        ab[:, hd * 256 + 128: hd * 256 + 256] = np.where(k <= q, -a * (q - k), NEG)
        if g == 2:
            j = hd - 8
            ab[:, 3072 + j * 128: 3072 + (j + 1) * 128] = np.where(k >= 64, -a * (q + 128 - k), NEG)
    return cst, ab


_CACHE = {}


def _get_program():
    if 'nc' not in _CACHE:
        reqs = build_program(None)
        nc, nops, cnt = build_program(reqs)
        _CACHE['nc'] = nc
    return _CACHE['nc']


def kernel(x, c, w_ada, b_ada, norm1_w, norm2_w, w_in, conv_w, conv_b, dt_bias, a_log, d_skip,
           ssd_norm_w, w_ssd_out, q_norm_w, k_norm_w, w_attn_out, w_pool_mix, pool_scale,
           w_pool_out, w_out, w_ff1, w_ff2):
    f = lambda a: np.asarray(a, np.float32)
    x, c, w_ada, b_ada, norm1_w, norm2_w, w_in = map(f, (x, c, w_ada, b_ada, norm1_w, norm2_w, w_in))
    conv_w, conv_b, dt_bias, a_log, d_skip, ssd_norm_w = map(f, (conv_w, conv_b, dt_bias, a_log, d_skip, ssd_norm_w))
    w_ssd_out, q_norm_w, k_norm_w, w_attn_out, w_pool_mix = map(f, (w_ssd_out, q_norm_w, k_norm_w, w_attn_out, w_pool_mix))
    pool_scale, w_pool_out, w_out, w_ff1, w_ff2 = map(f, (pool_scale, w_pool_out, w_out, w_ff1, w_ff2))

    offs = np.cumsum([0, 1024, 1536, 16, 1536, 1536, 1536, 1024, 3072])
    wst = np.zeros((L * 144, 128, 1024), np.float32)
    wao = np.zeros((L * 8, 128, 512), np.float32)
    wpm = np.zeros((L * 8, 128, 256), np.float32)
    wf2 = np.zeros((L * 16, 128, 2048), np.float32)
    wdt = np.zeros((L, 128, 128), np.float32)
    wada = np.zeros((L * 48, 128, 1024), np.float32)
    vecs = np.zeros((4, 128, L * VL + 8), np.float32)
    rows = np.zeros((128, L * 48), np.float32)
    for l in range(L):
        wi = w_in[l]
        parts = [wi[:, offs[i]:offs[i + 1]] for i in range(8)]
        base = l * 144
        wst[base + 136: base + 144] = _blk_st(parts[0])
        wst[base + 0: base + 12] = _blk_st(parts[1])
        wst[base + 12: base + 24] = _blk_st(parts[3])
        wst[base + 24: base + 36] = _blk_st(parts[4])
        wst[base + 36: base + 48] = _blk_st(parts[5])
        wst[base + 48: base + 56] = _blk_st(parts[6])
        wst[base + 56: base + 80] = _blk_st(parts[7])
        wst[base + 80: base + 88] = _blk_st(w_ssd_out[l])
        wst[base + 88: base + 96] = _blk_st(w_pool_out[l])
        wst[base + 96: base + 104] = _blk_st(w_out[l])
        wst[base + 104: base + 136] = _blk_st(w_ff1[l])
        wao[l * 8:(l + 1) * 8] = _blk_st(w_attn_out[l])
        for gi in range(4):
            wpm[l * 8 + gi * 2: l * 8 + gi * 2 + 2] = _blk_st(w_pool_mix[l, gi])
        f2 = _blk_st(w_ff2[l])
        wf2[l * 16:(l + 1) * 16] = f2.reshape(8, 128, 2, 2048).transpose(0, 2, 1, 3).reshape(16, 128, 2048)
        wdt[l] = _blk_st(parts[2], 16)[0]
        wada[l * 48:(l + 1) * 48] = _blk_st(w_ada[l])
        o = l * VL
        for b in range(4):
            v = vecs[b]
            v[:, o + 0:o + 8] = _pvec(norm1_w[l])
            v[:, o + 8:o + 16] = _pvec(norm2_w[l])
            v[:, o + 16:o + 64] = _pvec(b_ada[l])
            for k in range(4):
                v[:, o + 64 + k * 12: o + 64 + (k + 1) * 12] = _pvec(conv_w[l, k])
            v[:, o + 112:o + 124] = _pvec(conv_b[l])
            v[:, o + 124:o + 132] = _pvec(ssd_norm_w[l])
            v[:, o + 132:o + 140] = _pvec(pool_scale[l])
            v[:, o + 140] = q_norm_w[l]
            v[:, o + 141] = k_norm_w[l]
        rows[:, l * 48: l * 48 + 16] = dt_bias[l][None, :]
        rows[:, l * 48 + 16: l * 48 + 32] = a_log[l][None, :]
        rows[:, l * 48 + 32: l * 48 + 48] = d_skip[l][None, :]
    for b in range(4):
        vecs[b][:, L * VL: L * VL + 8] = _pvec(c[b])
    cst, ab = _consts()
    nc = _get_program()
    shared = dict(wst=wst, wao=wao, wpm=wpm, wf2=wf2, wdt=wdt, wada=wada, rows=rows, cst=cst, abias=ab)
    in_maps = []
    for core in range(N_CORES):
        m = dict(shared)
        m['x'] = np.ascontiguousarray(x[core])
        m['vecs'] = vecs[core]
        in_maps.append(m)
    res = run_bass_kernel_spmd(nc, in_maps, core_ids=list(range(N_CORES)))
    if DUMPS:
        for k_ in res.results[0]:
            if k_.startswith('dbg_'):
                LAST[k_] = np.asarray(res.results[0][k_])
    out = np.stack([np.asarray(res.results[b]['y'], np.float32) for b in range(4)], axis=0)
    return out
```

```python
import math
import numpy as np
import ml_dtypes
import concourse.bass as bass
import concourse.mybir as mybir
from concourse.bass_utils import run_bass_kernel_spmd

F32 = mybir.dt.float32
BF16 = mybir.dt.bfloat16
AF = mybir.ActivationFunctionType
ALU = mybir.AluOpType

L = 4
D = 1024
S = 4096
NQ = 1024
NQT = S // NQ
EPS = 1e-6
NEG = -30000.0
GROUPS = ((128, 1), (512, 4), (2048, 16))
VL = 142
ENG = ['pe', 'dve', 'act', 'pool', 'sp']
N_CORES = 4
DUMPS = set()
LAST = {}
STAGES = {'aq1', 'aq2', 'aq3', 'aq', 'av', 'au', 'g0', 'g1', 'g2', 'ada', 'n1', 'attn', 'ssd', 'pool', 'merge', 'n2', 'ffn'}


def _alibi_slopes(n):
    def pow2(k):
        start = 2.0 ** (-8.0 / k)
        return [start ** (i + 1) for i in range(k)]
    if math.log2(n).is_integer():
        s = pow2(n)
    else:
        c = 2 ** math.floor(math.log2(n))
        s = pow2(c) + pow2(2 * c)[0::2][: n - c]
    return np.sort(np.asarray(s, np.float32))[::-1].copy()


class Prog:
    def __init__(self, dry):
        self.dry = dry
        self.ops = []
        self.lastw = {}
        self.rd_eng = {}
        self.rd_dma = {}
        self.bank = 0

    def op(self, eng, fn, reads=(), writes=(), dsem=None):
        if self.dry:
            return
        reads = list(reads)
        writes = list(writes)
        if dsem is not None:
            writes.append(('sem', dsem))
        strong = set()
        war = set()
        excl = set()
        for k in reads:
            w = self.lastw.get(k)
            if w is not None:
                strong.add(w)
            if k[0] == 'ps':
                for r in self.rd_eng.get(k, {}).values():
                    excl.add(r)
        for k in writes:
            w = self.lastw.get(k)
            if w is not None:
                strong.add(w)
            for r in self.rd_eng.get(k, {}).values():
                war.add(r)
            for r in self.rd_dma.get(k, ()):
                war.add(r)
        i = len(self.ops)
        self.ops.append((eng, fn, strong, war, dsem, excl))
        for k in reads:
            if dsem is None:
                self.rd_eng.setdefault(k, {})[eng] = i
            else:
                self.rd_dma.setdefault(k, []).append(i)
        for k in writes:
            self.lastw[k] = i
            self.rd_eng[k] = {}
            self.rd_dma[k] = []

    def next_bank(self):
        b = self.bank
        self.bank = (self.bank + 1) % 8
        return b

    def emit(self, nc, stack):
        ops = self.ops
        n = len(ops)
        need = [None] * n
        signal = set()
        for i, (eng, fn, strong, war, dsem, excl) in enumerate(ops):
            deps = set()
            for d in strong:
                de, dd = ops[d][0], ops[d][4]
                if dd is None and dsem is None and de == eng and eng == 'pe':
                    continue
                deps.add(d)
            for d in war:
                de, dd = ops[d][0], ops[d][4]
                if dd is None and dsem is None and de == eng:
                    continue
                deps.add(d)
            for d in excl:
                de, dd = ops[d][0], ops[d][4]
                if dd is None and dsem is None and de == eng:
                    continue
                deps.add(d)
            need[i] = deps
            for d in deps:
                if ops[d][4] is None:
                    signal.add(d)
        cnt = {e: 0 for e in ENG}
        dcnt = {}
        ticket = {}
        semnames = set('E_' + e for e in ENG)
        for i, o in enumerate(ops):
            if o[4] is not None:
                dcnt[o[4]] = dcnt.get(o[4], 0) + 16
                ticket[i] = (o[4], dcnt[o[4]])
                semnames.add(o[4])
            elif i in signal:
                cnt[o[0]] += 1
                ticket[i] = ('E_' + o[0], cnt[o[0]])
        sems = {s: stack.enter_context(nc.semaphore(s)) for s in sorted(semnames)}
        per = {e: [i for i in range(n) if ops[i][0] == e] for e in ENG}

        def run(ename, eobj):
            seen = {}
            for i in per[ename]:
                waits = {}
                for d in need[i]:
                    s, v = ticket[d]
                    if waits.get(s, 0) < v:
                        waits[s] = v
                for s, v in waits.items():
                    if seen.get(s, 0) >= v:
                        continue
                    eobj.wait_ge(sems[s], v)
                    seen[s] = v
                ins = ops[i][1](eobj)
                if i in ticket and ins is not None:
                    ins.then_inc(sems[ticket[i][0]], 16 if ops[i][4] is not None else 1)

        with nc.Block() as block:
            @block.tensor
            def _(e):
                run('pe', e)

            @block.vector
            def _(e):
                run('dve', e)

            @block.scalar
            def _(e):
                run('act', e)

            @block.gpsimd
            def _(e):
                run('pool', e)

            @block.sync
            def _(e):
                run('sp', e)
        return cnt


NSLOT = 5
PF = 3
ARENA = 81920
GR = 1024


class Ctx:
    pass


def build_program(reqs_in):
    dry = reqs_in is None
    nc = bass.Bass("TRN2", target_bir_lowering=False)
    P = Prog(dry)
    from contextlib import ExitStack
    stack = ExitStack()

    def dram_in(name, shape, dt=F32):
        return nc.dram_tensor(name, list(shape), dt, kind="ExternalInput").ap()

    x_d = dram_in("x", [S, D])
    wst_d = dram_in("wst", [L * 144, 128, 1024])
    wao_d = dram_in("wao", [L * 8, 128, 512])
    wpm_d = dram_in("wpm", [L * 8, 128, 256])
    wf2_d = dram_in("wf2", [L * 16, 128, 2048])
    wdt_d = dram_in("wdt", [L, 128, 128])
    wada_d = dram_in("wada", [L * 48, 128, 1024])
    vecs_d = dram_in("vecs", [128, L * VL + 8])
    rows_d = dram_in("rows", [128, L * 48])
    cst_d = dram_in("cst", [128, 2624])
    ab_d = dram_in("abias", [128, 3584])
    y_d = nc.dram_tensor("y", [S, D], F32, kind="ExternalOutput").ap()
    kc_d = nc.dram_tensor("kcache", [L * 12, 128, S], BF16).ap()
    vc_d = nc.dram_tensor("vcache", [L * 12, S, 128], BF16).ap()

    def sb(name, n, dt):
        return stack.enter_context(nc.sbuf_tensor("s_" + name, [128, n], dt))

    xT = sb("xT", 8 * NQ, F32)
    hT = sb("hT", 8 * NQ, BF16)
    oT = sb("oT", 4 * NQ, BF16)
    wsl = [sb("wsl%d" % i, 2048, BF16) for i in range(NSLOT)]
    cst = sb("cst", 2624, F32)
    abias = sb("abias", 3584, F32)
    vecs = sb("vecs", L * VL + 8, F32)
    rows = sb("rows", L * 48, F32)
    modv = sb("modv", L * 48, F32)
    sv = sb("sv", L * 16 + 32, F32)
    Sst = sb("Sst", L * 1024, F32)
    chalo = sb("chalo", L * 36, F32)
    phalo = sb("phalo", L * 120, F32)
    cbf = sb("cbf", 512, BF16)
    cond2 = sb("cond2", 16, F32)
    wdt_sb = sb("wdtsb", L * 128, BF16)
    arena = sb("arena", ARENA // 2, BF16)
    PS = [stack.enter_context(nc.psum_tensor("ps%d" % i, [128, 512], F32)) for i in range(8)]

    x3 = xT[:, :].rearrange("p (c t) -> p c t", t=NQ)
    h3 = hT[:, :].rearrange("p (c t) -> p c t", t=NQ)
    o3 = oT[:, :].rearrange("p (c t) -> p c t", t=NQ)
    ident = cst[:, 0:128]
    tri = cst[:, 128:256]
    negm = cst[:, 256:384]
    ones = cst[:, 384:512]
    esel = cst[:, 512:2560]
    invc = cst[:, 2560:2624]
    ident_bf = cbf[:, 0:128]
    ones_bf = cbf[:, 128:256]
    tri_bf = cbf[:, 256:384]
    negm_bf = cbf[:, 384:512]

    def AR(off, n, dt):
        esz = 4 if dt == F32 else 2
        assert off % 4 == 0 and off + n * esz <= ARENA, (off, n)
        ap = arena[:, off // 2: off // 2 + n * esz // 2]
        if dt == F32:
            ap = ap.bitcast(F32)
        keys = [('ar', g) for g in range(off // GR, (off + n * esz - 1) // GR + 1)]
        return ap, keys

    def psk(b):
        return [('ps', b)]

    def mm(out, okeys, lhsT, lkeys, rhs, rkeys, start, stop):
        P.op('pe', lambda e: e.matmul(out, lhsT, rhs, start=start, stop=stop),
             list(lkeys) + list(rkeys), okeys)

    def tr(out, okeys, in_, ikeys, idn):
        P.op('pe', lambda e: e.transpose(out, in_, idn), list(ikeys) + ['const'], okeys)

    def act(out, okeys, in_, ikeys, func, bias=None, scale=None, extra=()):
        kw = {}
        if bias is not None:
            kw['bias'] = bias
        if scale is not None:
            kw['scale'] = scale
        P.op('act', lambda e: e.activation(out, in_, func, **kw), list(ikeys) + list(extra), okeys)

    def tt(out, okeys, a, akeys, b, bkeys, op, eng='dve'):
        P.op(eng, lambda e: e.tensor_tensor(out, a, b, op), list(akeys) + list(bkeys), okeys)

    def ts(out, okeys, a, akeys, s1, s2, op0, op1=None, extra=(), eng='dve'):
        if op1 is None:
            P.op(eng, lambda e: e.tensor_scalar(out, a, s1, None, op0), list(akeys) + list(extra), okeys)
        else:
            P.op(eng, lambda e: e.tensor_scalar(out, a, s1, s2, op0, op1), list(akeys) + list(extra), okeys)

    def stt(out, okeys, a, akeys, sc, b, bkeys, op0, op1, extra=()):
        P.op('dve', lambda e: e.scalar_tensor_tensor(out, a, sc, b, op0, op1),
             list(akeys) + list(bkeys) + list(extra), okeys)

    def cp(out, okeys, in_, ikeys, eng='dve'):
        P.op(eng, lambda e: e.tensor_copy(out, in_), ikeys, okeys)

    def dma(eng, out, okeys, in_, ikeys, sem, **kw):
        P.op(eng, lambda e: e.dma_start(out=out, in_=in_, **kw), ikeys, okeys, dsem=sem)

    def dump(name, ap, keys, n):
        if name not in DUMPS:
            return
        dt_ = nc.dram_tensor("dbg_" + name, [128, n], F32, kind="ExternalOutput").ap()
        dma('pool', dt_[:, :], [('dbg', name)], ap, keys, 'dbg', max_dma_last_dim=4096)

    reqs_out = []
    wstate = {'issued': 0, 'next': 0}

    wdram = {'wst': wst_d, 'wao': wao_d, 'wpm': wpm_d, 'wf2': wf2_d}

    def wissue(i):
        name, idx, n = reqs_in[i]
        src = wdram[name][idx]
        slot = i % NSLOT
        dma('pool', wsl[slot][:, 0:n], [('wsl', slot)], src, ['wdram'], 'ws%d' % slot,
            max_dma_last_dim=4096)

    def wnext(name, idx, n):
        if dry:
            reqs_out.append((name, idx, n))
            return wsl[0][:, 0:n], [('wsl', 0)]
        i = wstate['next']
        wstate['next'] += 1
        while wstate['issued'] < min(len(reqs_in), i + PF + 1):
            wissue(wstate['issued'])
            wstate['issued'] += 1
        slot = i % NSLOT
        return wsl[slot][:, 0:n], [('wsl', slot)]

    dma('sp', cst[:, :], ['const'], cst_d[:, :], [], 'cld')
    dma('sp', abias[:, :], ['const'], ab_d[:, :], [], 'cld')
    dma('sp', vecs[:, :], ['const'], vecs_d[:, :], [], 'cld')
    dma('sp', rows[:, :], ['const'], rows_d[:, :], [], 'cld')
    for l_ in range(L):
        dma('pool', wdt_sb[:, l_ * 128:(l_ + 1) * 128], ['const'], wdt_d[l_], [], 'cld2')
    cp(ident_bf, ['const'], ident, ['const'])
    cp(ones_bf, ['const'], ones, ['const'])
    cp(tri_bf, ['const'], tri, ['const'])
    cp(negm_bf, ['const'], negm, ['const'])
    P.op('dve', lambda e: e.memset(Sst[:, :], 0.0), [], ['Sst'])
    P.op('dve', lambda e: e.memset(chalo[:, :], 0.0), [], ['chalo'])
    P.op('dve', lambda e: e.memset(phalo[:, :], 0.0), [], ['phalo'])

    def vcol(l, off, n=1):
        return vecs[:, l * VL + off: l * VL + off + n]

    V_N1, V_N2, V_BADA, V_CW, V_CB, V_SNW, V_PS, V_QW, V_KW = 0, 8, 16, 64, 112, 124, 132, 140, 141

    cvec = vecs[:, L * VL: L * VL + 8]
    c2v = cond2[:, :].rearrange("p (k t) -> p k t", t=2)
    act(c2v[:, :, 0], ['cond'], cvec, ['const'], AF.Silu)
    act(c2v[:, :, 1], ['cond'], cvec, ['const'], AF.Silu)
    for l in range(L if 'ada' in STAGES else 0):
        b = P.next_bank()
        for fb in range(48):
            buf, bufk = AR((fb % 2) * 4096, 1024, F32)
            dma('sp', buf, bufk, wada_d[l * 48 + fb], ['wdram'], 'wf%d' % (fb % 2))
            w3 = buf.rearrange("p (k n) -> p k n", n=128)
            for kc in range(8):
                mm(PS[b][:, 2 * fb: 2 * fb + 2], psk(b), w3[:, kc, :], bufk,
                   c2v[:, kc, :], ['cond'], kc == 0, kc == 7)
        pv = PS[b][:, 0:96].rearrange("p (f t) -> p f t", t=2)
        tt(modv[:, l * 48:(l + 1) * 48], [('modv', l)], pv[:, :, 0], psk(b), vcol(l, V_BADA, 48), ['const'], ALU.add)
        stt(sv[:, l * 16: l * 16 + 8], [('sv', l)], modv[:, l * 48 + 8: l * 48 + 16], [('modv', l)], 1.0,
            vcol(l, V_N1, 8), ['const'], ALU.add, ALU.mult)
        stt(sv[:, l * 16 + 8: l * 16 + 16], [('sv', l)], modv[:, l * 48 + 32: l * 48 + 40], [('modv', l)], 1.0,
            vcol(l, V_N2, 8), ['const'], ALU.add, ALU.mult)

    def mod(l, which, dc):
        return modv[:, l * 48 + which * 8 + dc: l * 48 + which * 8 + dc + 1]

    def rmsnorm(l, which):
        sq, sqk = AR(0, 4096, F32)
        rs, rsk = AR(16384, 512, F32)
        tmp, tmpk = AR(18432, 512, F32)
        sq3 = sq.rearrange("p (c t) -> p c t", t=512)
        for s in range(2):
            tsl = slice(s * 512, (s + 1) * 512)
            for dc in range(8):
                act(sq3[:, dc, :], sqk, x3[:, dc, tsl], [('x', s)], AF.Square)
            b = P.next_bank()
            for dc in range(8):
                mm(PS[b][:, :], psk(b), ones, ['const'], sq3[:, dc, :], sqk, dc == 0, dc == 7)
            act(rs, rsk, PS[b][:, :], psk(b), AF.Ln, bias=EPS, scale=1.0 / D)
            act(rs, rsk, rs, rsk, AF.Exp, scale=-0.5)
            for dc in range(8):
                tt(tmp, tmpk, x3[:, dc, tsl], [('x', s)], rs, rsk, ALU.mult)
                scl = sv[:, l * 16 + which * 8 + dc: l * 16 + which * 8 + dc + 1]
                act(h3[:, dc, tsl], [('h', s)], tmp, tmpk, AF.Identity,
                    bias=mod(l, 0 if which == 0 else 3, dc), scale=scl, extra=[('sv', l), ('modv', l)])

    def wst_blk(l, idx):
        return ('wst', l * 144 + idx)

    W_XBC, W_Q, W_K, W_V, W_U, W_G, W_SO, W_PO, W_WO, W_F1, W_Z = 0, 12, 24, 36, 48, 56, 80, 88, 96, 104, 136

    slopes = _alibi_slopes(12)

    def attention(l, qt):
        for jp in (0, 2):
            gens = [head_gen(l, qt, jp, 0), head_gen(l, qt, jp + 1, 1)]
            alive = [True, True]
            while any(alive):
                for gi_ in range(2):
                    if alive[gi_]:
                        try:
                            next(gens[gi_])
                        except StopIteration:
                            alive[gi_] = False

    def head_gen(l, qt, j, par):
        PB = par * 40960

        def A(off, n, dt):
            return AR(PB + off, n, dt)
        acc, acck = A(16384, 2048, F32)
        acc3 = acc.rearrange("p (a t) -> p a t", t=NQ)
        sq, sqk = A(24576, 512, BF16)
        qraw, qrk = A(25600, 512, F32)
        rstd, rsk = A(27648, 512, F32)
        tmpS = [A(29696, 512, F32), A(31744, 512, F32)]
        PT = [A(33792, 512, BF16), A(34816, 512, BF16)]
        rec, reck = A(35840, 1024, F32)
        for g in range(3):
            if ('g%d' % g) not in STAGES:
                continue
            d = GROUPS[g][1]
            hd = 4 * g + j
            npr = NQ // d
            nq = min(128, npr)
            upr = npr // nq
            U = d * upr
            m0 = npr * qt
            Qp, Qk = A(0, 1024, BF16)
            Kp, Kk = A(2048, 1024, BF16)
            Kh, Khk = A(4096, d * 128, BF16)
            Vp, Vk = A(8192, U * 128, BF16)
            Vh, Vhk = A(12288, d * 128, BF16)
            VT, VTk = A(29696, 1024, BF16)
            Qp3 = Qp.rearrange("p (u i) -> p u i", i=nq)
            Kp3 = Kp.rearrange("p (u i) -> p u i", i=nq)
            Kh3 = Kh.rearrange("p (r i) -> p r i", i=128)
            Vp3 = Vp.rearrange("p (u e) -> p u e", e=128)
            Vh3 = Vh.rearrange("p (r e) -> p r e", e=128)
            for which, dst3, dk, widx, wv in ((0, Qp3, Qk, W_Q, V_QW), (1, Kp3, Kk, W_K, V_KW)):
                w, wk = wnext(*wst_blk(l, widx + hd), 1024)
                w3 = w.rearrange("p (k n) -> p k n", n=128)
                for s in range(2):
                    tsl = slice(s * 512, (s + 1) * 512)
                    b = P.next_bank()
                    for kc in range(8):
                        mm(PS[b][:, :], psk(b), w3[:, kc, :], wk, h3[:, kc, tsl], [('h', s)], kc == 0, kc == 7)
                    act(sq, sqk, PS[b][:, :], psk(b), AF.Square)
                    cp(qraw, qrk, PS[b][:, :], psk(b))
                    b2 = P.next_bank()
                    mm(PS[b2][:, :], psk(b2), ones_bf, ['const'], sq, sqk, True, True)
                    act(rstd, rsk, PS[b2][:, :], psk(b2), AF.Ln, bias=EPS, scale=1.0 / 128)
                    act(rstd, rsk, rstd, rsk, AF.Exp, scale=-0.5)
                    mpt = 512 // d
                    if d == 1:
                        dst = dst3[:, s * 4:(s + 1) * 4, :]
                        src_q = qraw.rearrange("p (u i) -> p u i", i=128)
                        src_r = rstd.rearrange("p (u i) -> p u i", i=128)
                    else:
                        flat = dst3.rearrange("p u i -> p (u i)").rearrange("p (r m) -> p r m", m=npr)
                        dst = flat[:, :, s * mpt:(s + 1) * mpt]
                        src_q = qraw.rearrange("p (m r) -> p r m", r=d)
                        src_r = rstd.rearrange("p (m r) -> p r m", r=d)
                    stt(dst, dk, src_q, qrk, vcol(l, wv), src_r, rsk, ALU.mult, ALU.mult, extra=['const'])
                    yield
            w, wk = wnext(*wst_blk(l, W_V + hd), 1024)
            w3 = w.rearrange("p (k n) -> p k n", n=128)
            for s in range(2):
                tsl = slice(s * 512, (s + 1) * 512)
                b = P.next_bank()
                for kc in range(8):
                    mm(PS[b][:, :], psk(b), w3[:, kc, :], wk, h3[:, kc, tsl], [('h', s)], kc == 0, kc == 7)
                act(VT[:, tsl], VTk, PS[b][:, :], psk(b), AF.Copy)
                yield
            for u0 in range(0, U, 8):
                b = P.next_bank()
                psb = PS[b][:, :].bitcast(BF16)
                for uu in range(8):
                    u = u0 + uu
                    r, nb = u // upr, u % upr
                    t0 = r + d * nb * nq
                    tr(psb[0:nq, uu * 128:(uu + 1) * 128], psk(b), VT[:, t0: t0 + d * (nq - 1) + 1: d], VTk, ident_bf)
                act(Vp3[0:nq, u0:u0 + 8, :], Vk, psb[0:nq, :].rearrange("p (u e) -> p u e", e=128), psk(b), AF.Copy)
                yield
            kcv = kc_d[l * 12 + hd].rearrange("p (r m) -> p r m", m=S // d)
            vcv = vc_d[l * 12 + hd].rearrange("(r m) e -> m r e", m=S // d)
            if qt < NQT - 1:
                dma('sp', kcv[:, :, m0:m0 + npr], [('kc', l, hd)],
                    Kp.rearrange("p (r m) -> p r m", m=npr), Kk, 'kst')
                for nb in range(upr):
                    dma('sp', vcv[m0 + nb * nq: m0 + (nb + 1) * nq, :, :], [('vc', l, hd)],
                        Vp3[0:nq, nb:U:upr, :], Vk, 'vst')
            have_prev = qt > 0
            pm0 = max(0, m0 - 128)
            npv = m0 - pm0
            if have_prev:
                if npv < 128:
                    P.op('dve', lambda e: e.memset(Kh, 0.0), [], Khk)
                    P.op('dve', lambda e: e.memset(Vh, 0.0), [], Vhk)
                dma('sp', Kh3[:, :, 128 - npv:128], Khk, kcv[:, :, pm0:m0], [('kc', l, hd)], 'kld')
                dma('sp', Vh3[128 - npv:128, :, :], Vhk, vcv[pm0:m0, :, :], [('vc', l, hd)], 'vld')
            yield
            sc = 128.0 ** -0.5
            abh = abias[:, hd * 256:(hd + 1) * 256]
            if g == 2 and qt == 1:
                ab_prev = abias[:, 3072 + j * 128: 3072 + (j + 1) * 128]
            else:
                ab_prev = abh[:, 0:128]
            ab_cur = abh[:, 128:256]
            for u in range(U):
                r, nb = u // upr, u % upr
                prev = None
                if nb >= 1:
                    prev = (Kp3[:, u - 1, :], Kk, Vp3[:, u - 1, :], Vk, abh[:, 0:128])
                elif have_prev:
                    prev = (Kh3[:, r, :], Khk, Vh3[:, r, :], Vhk, ab_prev)
                b = P.next_bank()
                (tS, tSk), (pT, pTk) = tmpS[u % 2], PT[u % 2]
                if prev is not None:
                    mm(PS[b][:, 0:nq], psk(b), prev[0], prev[1], Qp3[:, u, :], Qk, True, True)
                mm(PS[b][0:nq, 128:128 + nq], psk(b), Kp3[:, u, :], Kk, Qp3[:, u, :], Qk, True, True)
                if prev is not None:
                    stt(tS[:, 0:nq], tSk, PS[b][:, 0:nq], psk(b), sc, prev[4][:, 0:nq], ['const'],
                        ALU.mult, ALU.add)
                stt(tS[0:nq, 128:128 + nq], tSk, PS[b][0:nq, 128:128 + nq], psk(b), sc,
                    ab_cur[0:nq, 0:nq], ['const'], ALU.mult, ALU.add)
                if prev is not None:
                    act(pT[:, 0:nq], pTk, tS[:, 0:nq], tSk, AF.Exp)
                act(pT[0:nq, 128:128 + nq], pTk, tS[0:nq, 128:128 + nq], tSk, AF.Exp)
                yield
                b2 = P.next_bank()
                if prev is not None:
                    mm(PS[b2][:, 0:nq], psk(b2), prev[2], prev[3], pT[:, 0:nq], pTk, True, False)
                mm(PS[b2][:, 0:nq], psk(b2), Vp3[0:nq, u, :], Vk, pT[0:nq, 128:128 + nq], pTk,
                   prev is None, True)
                if prev is not None:
                    mm(PS[b2][:, 128:128 + nq], psk(b2), ones_bf, ['const'], pT[:, 0:nq], pTk, True, False)
                mm(PS[b2][:, 128:128 + nq], psk(b2), ones_bf[0:nq, :], ['const'], pT[0:nq, 128:128 + nq], pTk,
                   prev is None, True)
                t0 = r + d * nb * nq
                dsta = acc3[:, :, t0: t0 + d * (nq - 1) + 1: d]
                srcp = PS[b2][:, 0:256].rearrange("p (a q) -> p a q", q=128)[:, :, 0:nq]
                if g == 0:
                    cp(dsta, acck, srcp, psk(b2))
                else:
                    tt(dsta, acck, srcp, psk(b2), dsta, acck, ALU.add)
                yield
        act(rec, reck, acc3[:, 1, :], acck, AF.Ln)
        act(rec, reck, rec, reck, AF.Exp, scale=-1.0)
        tt(o3[:, j, :], ['o'], acc3[:, 0, :], acck, rec, reck, ALU.mult)
        yield

    def ssd(l, qt, s, ynT3, ynk):
        tsl = slice(s * 512, (s + 1) * 512)
        xbc, xbck = AR(0, 12 * 512, BF16)
        xbc3 = xbc.rearrange("p (c t) -> p c t", t=512)
        raw, rawk = AR(12288, 516, F32)
        cacc, cak = AR(14592, 512, F32)
        sm, smk = AR(16896, 1536, F32)
        X, Xk = AR(23040, 1024, BF16)
        Xd, Xdk = AR(25088, 1024, BF16)
        Btm, Btk = AR(27136, 256, BF16)
        zs, zsk = AR(28160, 1024, F32)
        yv, yk = AR(32256, 1024, F32)
        yo, yok = AR(36352, 1024, F32)
        MT, MTk = AR(40448, 2048, BF16)
        acT, acTk = AR(44544, 128, F32)
        nacT, nacTk = AR(45568, 128, F32)
        cbT, cbTk = AR(46592, 256, F32)
        ynr, ynrk = AR(47616, 1024, BF16)
        Sbf, Sbfk = AR(49664, 1024, BF16)
        dec, deck = AR(51712, 512, F32)
        for c in range(12):
            w, wk = wnext(*wst_blk(l, W_XBC + c), 1024)
            w3 = w.rearrange("p (k n) -> p k n", n=128)
            b = P.next_bank()
            for kc in range(8):
                mm(PS[b][:, :], psk(b), w3[:, kc, :], wk, h3[:, kc, tsl], [('h', s)], kc == 0, kc == 7)
            hal = chalo[:, l * 36 + c * 3: l * 36 + c * 3 + 3]
            cp(raw[:, 0:3], rawk, hal, ['chalo'])
            act(raw[:, 3:515], rawk, PS[b][:, :], psk(b), AF.Copy)
            cp(hal, ['chalo'], raw[:, 512:515], rawk)
            ts(cacc, cak, raw[:, 0:512], rawk, vcol(l, V_CW + 0 * 12 + c), vcol(l, V_CB + c), ALU.mult, ALU.add,
               extra=['const'])
            for k in range(1, 4):
                stt(cacc, cak, raw[:, k:k + 512], rawk, vcol(l, V_CW + k * 12 + c), cacc, cak, ALU.mult, ALU.add,
                    extra=['const'])
            act(xbc3[:, c, :], xbck, cacc, cak, AF.Silu)
        zT, zTk = AR(53760, 8 * 512, BF16)
        zT3 = zT.rearrange("p (c t) -> p c t", t=512)
        for c in range(8):
            w, wk = wnext(*wst_blk(l, W_Z + c), 1024)
            w3 = w.rearrange("p (k n) -> p k n", n=128)
            b = P.next_bank()
            for kc in range(8):
                mm(PS[b][:, :], psk(b), w3[:, kc, :], wk, h3[:, kc, tsl], [('h', s)], kc == 0, kc == 7)
            act(zT3[:, c, :], zTk, PS[b][:, :], psk(b), AF.Silu)
        wdt3 = wdt_sb[:, l * 128:(l + 1) * 128].rearrange("p (k n) -> p k n", n=16)
        wdtk = ['const']
        S_l = Sst[:, l * 1024:(l + 1) * 1024]
        r_dtb = rows[:, l * 48: l * 48 + 16]
        r_alog = rows[:, l * 48 + 16: l * 48 + 32]
        r_dsk = rows[:, l * 48 + 32: l * 48 + 48]
        def smv(i):
            return sm[:, i * 16:(i + 1) * 16]
        for ch in range(4):
            c0 = s * 512 + ch * 128
            cl = slice(ch * 128, (ch + 1) * 128)
            first = (qt == 0 and s == 0 and ch == 0)
            b = P.next_bank()
            for kc in range(8):
                mm(PS[b][:, 0:16], psk(b), h3[:, kc, c0:c0 + 128], [('h', s)], wdt3[:, kc, :], wdtk, kc == 0, kc == 7)
            xr, ab_, e1, dtv, negA, dtA, acl, eac, wv_, acu, dcb = [smv(i) for i in range(11)]
            tt(xr, smk, PS[b][:, 0:16], psk(b), r_dtb, ['const'], ALU.add)
            stt(ab_, smk, xr, smk, -1.0, xr, smk, ALU.mult, ALU.max)
            act(e1, smk, ab_, smk, AF.Exp, scale=-1.0)
            act(e1, smk, e1, smk, AF.Ln, bias=1.0)
            stt(dtv, smk, xr, smk, 0.0, e1, smk, ALU.max, ALU.add)
            act(negA, smk, r_alog, ['const'], AF.Exp)
            stt(dtA, smk, dtv, smk, -1.0, negA, smk, ALU.mult, ALU.mult)
            b = P.next_bank()
            mm(PS[b][:, 0:16], psk(b), tri, ['const'], dtA, smk, True, True)
            mm(PS[b][:, 16:32], psk(b), ones, ['const'], dtA, smk, True, True)
            mm(PS[b][0:16, 128:256], psk(b), dtA, smk, tri, ['const'], True, True)
            cp(acu, smk, PS[b][:, 0:16], psk(b))
            cp(acl, smk, PS[b][:, 16:32], psk(b))
            cp(acT[0:16, :], acTk, PS[b][0:16, 128:256], psk(b))
            ts(nacT[0:16, :], nacTk, PS[b][0:16, 128:256], psk(b), -1.0, None, ALU.mult)
            act(eac, smk, acu, smk, AF.Exp)
            tt(wv_, smk, acl, smk, acu, smk, ALU.subtract)
            act(wv_, smk, wv_, smk, AF.Exp)
            tt(wv_, smk, wv_, smk, dtv, smk, ALU.mult)
            act(dcb, smk, acl, smk, AF.Exp)
            b = P.next_bank()
            psb = PS[b][:, :].bitcast(BF16)
            for c in range(8):
                tr(psb[:, c * 128:(c + 1) * 128], psk(b), xbc3[:, c, cl], xbck, ident_bf)
            b3 = P.next_bank()
            psb3 = PS[b3][:, :].bitcast(BF16)
            for gg in range(2):
                tr(psb3[:, gg * 128:(gg + 1) * 128], psk(b3), xbc3[:, 8 + gg, cl], xbck, ident_bf)
            cp(Btm, Btk, psb3[:, 0:256], psk(b3))
            xs3 = psb.rearrange("p (h q) -> p h q", q=64)
            X3 = X.rearrange("p (h q) -> p h q", q=64)
            Xd3 = Xd.rearrange("p (h q) -> p h q", q=64)
            yo3 = yo.rearrange("p (h q) -> p h q", q=64)
            tt(X3, Xk, xs3, psk(b), dtv.unsqueeze(2).to_broadcast([128, 16, 64]), smk, ALU.mult)
            tt(Xd3, Xdk, xs3, psk(b), wv_.unsqueeze(2).to_broadcast([128, 16, 64]), smk, ALU.mult)
            tt(yo3, yok, xs3, psk(b), r_dsk.unsqueeze(2).to_broadcast([128, 16, 64]), ['const'], ALU.mult)
            b = P.next_bank()
            for gg in range(2):
                mm(PS[b][:, gg * 128:(gg + 1) * 128], psk(b), xbc3[:, 8 + gg, cl], xbck, xbc3[:, 10 + gg, cl], xbck,
                   True, True)
            cb3 = cbT.rearrange("p (g q) -> p g q", q=128)
            tt(cb3, cbTk, PS[b][:, 0:256].rearrange("p (g q) -> p g q", q=128), psk(b),
               tri.unsqueeze(1).to_broadcast([128, 2, 128]), ['const'], ALU.mult)
            MT3 = MT.rearrange("p (h q) -> p h q", q=128)
            for h0 in range(0, 16, 4):
                b = P.next_bank()
                for hh in range(4):
                    h = h0 + hh
                    o_ = PS[b][:, hh * 128:(hh + 1) * 128]
                    es = esel[0:16, h * 128:(h + 1) * 128]
                    mm(o_, psk(b), es, ['const'], acT[0:16, :], acTk, True, False)
                    mm(o_, psk(b), nacT[0:16, :], nacTk, es, ['const'], False, False)
                    mm(o_, psk(b), ident_bf, ['const'], negm_bf, ['const'], False, True)
                act(dec, deck, PS[b][:, :], psk(b), AF.Exp)
                gg = h0 // 8
                tt(MT3[:, h0:h0 + 4, :], MTk, dec.rearrange("p (h q) -> p h q", q=128), deck,
                   cb3[:, gg, :].unsqueeze(1).to_broadcast([128, 4, 128]), cbTk, ALU.mult)
            by = [P.next_bank(), P.next_bank()]
            for h in range(16):
                mm(PS[by[h // 8]][:, (h % 8) * 64:(h % 8 + 1) * 64], psk(by[h // 8]), MT3[:, h, :], MTk,
                   X3[:, h, :], Xk, True, True)
            if not first:
                cp(Sbf, Sbfk, S_l, ['Sst'], eng='pool')
                bo = [P.next_bank(), P.next_bank()]
                for gg in range(2):
                    mm(PS[bo[gg]][:, :], psk(bo[gg]), xbc3[:, 10 + gg, cl], xbck, Sbf[:, gg * 512:(gg + 1) * 512],
                       Sbfk, True, True)
                yv3 = yv.rearrange("p (h q) -> p h q", q=64)
                for gg in range(2):
                    tt(yv3[:, gg * 8:(gg + 1) * 8, :], yk, PS[bo[gg]][:, :].rearrange("p (h q) -> p h q", q=64),
                       psk(bo[gg]), eac[:, gg * 8:(gg + 1) * 8].unsqueeze(2).to_broadcast([128, 8, 64]), smk,
                       ALU.mult)
                tt(yo, yok, yo, yok, yv, yk, ALU.add, eng='pool')
            for hb in range(2):
                tt(yv[:, hb * 512:(hb + 1) * 512], yk, PS[by[hb]][:, :], psk(by[hb]),
                   yo[:, hb * 512:(hb + 1) * 512], yok, ALU.add)
            bs = [P.next_bank(), P.next_bank()]
            for gg in range(2):
                mm(PS[bs[gg]][:, :], psk(bs[gg]), Btm[:, gg * 128:(gg + 1) * 128], Btk,
                   Xd[:, gg * 512:(gg + 1) * 512], Xdk, True, True)
            S3 = S_l.rearrange("p (h q) -> p h q", q=64)
            if not first:
                tt(S3, ['Sst'], S3, ['Sst'], dcb.unsqueeze(2).to_broadcast([128, 16, 64]), smk, ALU.mult)
                for gg in range(2):
                    tt(S_l[:, gg * 512:(gg + 1) * 512], ['Sst'], PS[bs[gg]][:, :], psk(bs[gg]),
                       S_l[:, gg * 512:(gg + 1) * 512], ['Sst'], ALU.add)
            else:
                for gg in range(2):
                    cp(S_l[:, gg * 512:(gg + 1) * 512], ['Sst'], PS[bs[gg]][:, :], psk(bs[gg]))
            bz = P.next_bank()
            pz = PS[bz][:, :].bitcast(BF16)
            for c in range(8):
                tr(pz[:, c * 128:(c + 1) * 128], psk(bz), zT3[:, c, cl], zTk, ident_bf)
            cp(zs, zsk, pz, psk(bz))
            tt(yv, yk, yv, yk, zs, zsk, ALU.mult)
            ssq = sm[:, 176:178]
            for gg in range(2):
                P.op('act', (lambda gg=gg: lambda e: e.activation(zs[:, gg * 512:(gg + 1) * 512],
                                                                  yv[:, gg * 512:(gg + 1) * 512], AF.Square,
                                                                  accum_out=ssq[:, gg:gg + 1]))(),
                     yk, zsk + smk)
            act(ssq, smk, ssq, smk, AF.Ln, bias=EPS, scale=1.0 / 512)
            act(ssq, smk, ssq, smk, AF.Exp, scale=-0.5)
            for gg in range(2):
                ts(ynr[:, gg * 512:(gg + 1) * 512], ynrk, yv[:, gg * 512:(gg + 1) * 512], yk, ssq[:, gg:gg + 1],
                   None, ALU.mult, extra=smk)
            b = P.next_bank()
            psb = PS[b][:, :].bitcast(BF16)
            for c in range(8):
                tr(psb[:, c * 128:(c + 1) * 128], psk(b), ynr[:, c * 128:(c + 1) * 128], ynrk, ident_bf)
            for c in range(8):
                act(ynT3[:, c, c0:c0 + 128], ynk, psb[:, c * 128:(c + 1) * 128], psk(b), AF.Identity,
                    scale=vcol(l, V_SNW + c), extra=['const'])

    def pool(l, qt, pm3, pmk):
        pooled, plk = AR(16384, 8 * NQ, BF16)
        pl3 = pooled.rearrange("p (c t) -> p c t", t=NQ)
        ubA, uak = AR(0, 1040, F32)
        ubB, ubk = AR(4160, 1040, F32)
        ubC, uck = AR(8320, 1040, F32)
        for c in range(8):
            gi = c // 2
            w_ = 2 << gi
            w, wk = wnext(*wst_blk(l, W_U + c), 1024)
            w3 = w.rearrange("p (k n) -> p k n", n=128)
            hal = phalo[:, l * 120 + c * 15: l * 120 + c * 15 + 15]
            cp(ubA[:, 0:15], uak, hal, ['phalo'])
            for s in range(2):
                b = P.next_bank()
                for kc in range(8):
                    mm(PS[b][:, :], psk(b), w3[:, kc, :], wk, h3[:, kc, s * 512:(s + 1) * 512], [('h', s)],
                       kc == 0, kc == 7)
                act(ubA[:, 15 + s * 512: 15 + (s + 1) * 512], uak, PS[b][:, :], psk(b), AF.Copy)
            cp(hal, ['phalo'], ubA[:, 1024:1039], uak)
            src, srck = ubA, uak
            bufs = [(ubB, ubk), (ubC, uck)]
            step = 1
            i = 0
            while step < w_:
                dst, dstk = bufs[i % 2]
                tt(dst[:, step:1039], dstk, src[:, step:1039], srck, src[:, 0:1039 - step], srck, ALU.add,
                   eng='pool')
                src, srck = dst, dstk
                step *= 2
                i += 1
            stt(pl3[:, c, :], plk, src[:, 15:1039], srck, 1.0 / w_, ubA[:, 15:1039], uak, ALU.mult, ALU.subtract)
            if qt == 0:
                tmp16 = sv[:, L * 16: L * 16 + 16]
                tt(tmp16, ['svtmp'], src[:, 15:31], srck, invc[:, gi * 16:(gi + 1) * 16], ['const'], ALU.mult)
                tt(pl3[:, c, 0:16], plk, tmp16, ['svtmp'], ubA[:, 15:31], uak, ALU.subtract)
        for gi in range(4):
            for oc in range(2):
                w, wk = wnext('wpm', l * 8 + gi * 2 + oc, 256)
                w3 = w.rearrange("p (k n) -> p k n", n=128)
                for s in range(2):
                    b = P.next_bank()
                    for ic in range(2):
                        mm(PS[b][:, :], psk(b), w3[:, ic, :], wk, pl3[:, gi * 2 + ic, s * 512:(s + 1) * 512], plk,
                           ic == 0, ic == 1)
                    act(pm3[:, gi * 2 + oc, s * 512:(s + 1) * 512], pmk, PS[b][:, :], psk(b), AF.Identity,
                        scale=vcol(l, V_PS + gi * 2 + oc), extra=['const'])

    def merge(l, ynT3, ynk, pm3, pmk):
        mT, mTk = AR(0, 8 * NQ, BF16)
        mT3 = mT.rearrange("p (c t) -> p c t", t=NQ)
        sig, sigk = AR(16384, 512, F32)
        m1, m1k = AR(18432, 512, F32)
        m2, m2k = AR(20480, 512, F32)
        macc, mack = AR(22528, 1024, F32)
        srcs = ((W_SO, ynT3, ynk, 8), (None, o3, ['o'], 4), (W_PO, pm3, pmk, 8))
        for dc in range(8):
            for br, (widx, src3, srck, nk) in enumerate(srcs):
                if widx is not None:
                    wy = wnext(*wst_blk(l, widx + dc), 1024)
                else:
                    wy = wnext('wao', l * 8 + dc, 512)
                wg = wnext(*wst_blk(l, W_G + br * 8 + dc), 1024)
                wy3 = wy[0].rearrange("p (k n) -> p k n", n=128)
                wg3 = wg[0].rearrange("p (k n) -> p k n", n=128)
                for s in range(2):
                    tsl = slice(s * 512, (s + 1) * 512)
                    b1 = P.next_bank()
                    for kc in range(nk):
                        mm(PS[b1][:, :], psk(b1), wy3[:, kc, :], wy[1], src3[:, kc, tsl], srck, kc == 0, kc == nk - 1)
                    b2 = P.next_bank()
                    for kc in range(8):
                        mm(PS[b2][:, :], psk(b2), wg3[:, kc, :], wg[1], h3[:, kc, tsl], [('h', s)], kc == 0, kc == 7)
                    act(sig, sigk, PS[b2][:, :], psk(b2), AF.Sigmoid)
                    if br == 0:
                        tt(macc[:, tsl], mack, PS[b1][:, :], psk(b1), sig, sigk, ALU.mult)
                    elif br == 1:
                        tt(m2, m2k, PS[b1][:, :], psk(b1), sig, sigk, ALU.mult)
                        tt(macc[:, tsl], mack, macc[:, tsl], mack, m2, m2k, ALU.add, eng='pool')
                    else:
                        tt(m2, m2k, PS[b1][:, :], psk(b1), sig, sigk, ALU.mult)
                        tt(mT3[:, dc, tsl], mTk, macc[:, tsl], mack, m2, m2k, ALU.add, eng='pool')
        if l == 0:
            dump('mT', mT, mTk, 8 * NQ)
        for dc in range(8):
            w, wk = wnext(*wst_blk(l, W_WO + dc), 1024)
            w3 = w.rearrange("p (k n) -> p k n", n=128)
            for s in range(2):
                tsl = slice(s * 512, (s + 1) * 512)
                b = P.next_bank()
                for kc in range(8):
                    mm(PS[b][:, :], psk(b), w3[:, kc, :], wk, mT3[:, kc, tsl], mTk, kc == 0, kc == 7)
                stt(x3[:, dc, tsl], [('x', s)], PS[b][:, :], psk(b), mod(l, 2, dc), x3[:, dc, tsl], [('x', s)],
                    ALU.mult, ALU.add, extra=[('modv', l)])

    def ffn(l):
        hid, hidk = AR(0, 32 * NQ, BF16)
        hid3 = hid.rearrange("p (c t) -> p c t", t=NQ)
        sqv, sqk = AR(65536, 512, F32)
        sqv2, sqk2 = AR(67584, 512, F32)
        for fc in range(32):
            w, wk = wnext(*wst_blk(l, W_F1 + fc), 1024)
            w3 = w.rearrange("p (k n) -> p k n", n=128)
            for s in range(2):
                tsl = slice(s * 512, (s + 1) * 512)
                b = P.next_bank()
                for kc in range(8):
                    mm(PS[b][:, :], psk(b), w3[:, kc, :], wk, h3[:, kc, tsl], [('h', s)], kc == 0, kc == 7)
                sq_, sk_ = (sqv, sqk) if s == 0 else (sqv2, sqk2)
                act(sq_, sk_, PS[b][:, :], psk(b), AF.Square)
                hk = [('ar', (fc * NQ * 2 + s * 1024) // GR)]
                stt(hid3[:, fc, tsl], hk, PS[b][:, :], psk(b), 0.0, sq_, sk_, ALU.is_gt, ALU.mult)
        for dc in range(8):
            ws_ = [wnext('wf2', l * 16 + dc * 2 + hh, 2048) for hh in range(2)]
            for s in range(2):
                tsl = slice(s * 512, (s + 1) * 512)
                b = P.next_bank()
                for fc in range(32):
                    w3 = ws_[fc // 16][0].rearrange("p (k n) -> p k n", n=128)
                    hk = [('ar', (fc * NQ * 2 + s * 1024) // GR)]
                    mm(PS[b][:, :], psk(b), w3[:, fc % 16, :], ws_[fc // 16][1], hid3[:, fc, tsl], hk,
                       fc == 0, fc == 31)
                stt(x3[:, dc, tsl], [('x', s)], PS[b][:, :], psk(b), mod(l, 5, dc), x3[:, dc, tsl], [('x', s)],
                    ALU.mult, ALU.add, extra=[('modv', l)])

    for qt in range(NQT):
        for c in range(8):
            st, stk = AR((c % 2) * 4096, 1024, F32)
            dma('sp', st, stk, x_d[qt * NQ + c * 128: qt * NQ + (c + 1) * 128, :], [],
                'xs%d' % (c % 2))
            for half in range(2):
                b = P.next_bank()
                for dd in range(4):
                    dc = half * 4 + dd
                    tr(PS[b][:, dd * 128:(dd + 1) * 128], psk(b), st[:, dc * 128:(dc + 1) * 128], stk,
                       ident)
                P.op('act', (lambda b=b, half=half, c=c: lambda e: e.activation(
                    x3[:, half * 4:(half + 1) * 4, c * 128:(c + 1) * 128],
                    PS[b][:, :].rearrange("p (a q) -> p a q", q=128), AF.Copy))(),
                     psk(b), [('x', c // 4)])
        for l in range(L):
            if 'n1' in STAGES:
                rmsnorm(l, 0)
            if l == 0 and qt == 0:
                dump('hT', hT[:, :], [('h', 0), ('h', 1)], 8 * NQ)
            if 'attn' in STAGES:
                attention(l, qt)
            if l == 0 and qt == 0:
                dump('oT', oT[:, :], ['o'], 4 * NQ)
            ynT, ynk = AR(ARENA - 16384, 8 * NQ, BF16)
            ynT3 = ynT.rearrange("p (c t) -> p c t", t=NQ)
            if 'ssd' in STAGES:
                for s in range(2):
                    ssd(l, qt, s, ynT3, ynk)
            pm, pmk = AR(32768, 8 * NQ, BF16)
            pm3 = pm.rearrange("p (c t) -> p c t", t=NQ)
            if l == 0 and qt == 0:
                dump('ynT', ynT, ynk, 8 * NQ)
            if 'pool' in STAGES:
                pool(l, qt, pm3, pmk)
            if l == 0 and qt == 0:
                dump('pm', pm, pmk, 8 * NQ)
            if 'merge' in STAGES:
                merge(l, ynT3, ynk, pm3, pmk)
            if 'n2' in STAGES:
                rmsnorm(l, 1)
            if 'ffn' in STAGES:
                ffn(l)
        for c in range(8):
            st, stk = AR((c % 2) * 4096, 1024, F32)
            for half in range(2):
                b = P.next_bank()
                for dd in range(4):
                    dc = half * 4 + dd
                    tr(PS[b][:, dd * 128:(dd + 1) * 128], psk(b), x3[:, dc, c * 128:(c + 1) * 128], [('x', c // 4)],
                       ident)
                act(st[:, half * 512:(half + 1) * 512], stk, PS[b][:, :], psk(b), AF.Copy)
            dma('sp', y_d[qt * NQ + c * 128: qt * NQ + (c + 1) * 128, :], [('y', qt, c)], st,
                stk, 'xs%d' % (c % 2))
    P.op('sp', lambda e: None, [('sem', 'xs0'), ('sem', 'xs1')] + ([('sem', 'dbg')] if DUMPS else []), [])

    if dry:
        stack.close()
        return reqs_out
    cnt = P.emit(nc, stack)
    stack.close()
    return nc, len(P.ops), cnt


def _blk_st(W, ncol=128):
    K, N = W.shape
    return np.ascontiguousarray(
        W.reshape(K // 128, 128, N // ncol, ncol).transpose(2, 1, 0, 3).reshape(N // ncol, 128, (K // 128) * ncol))


def _pvec(v):
    return np.ascontiguousarray(v.reshape(-1, 128).T)


def _consts():
    cst = np.zeros((128, 2624), np.float32)
    i = np.arange(128)
    cst[:, 0:128] = np.eye(128)
    cst[:, 128:256] = (i[:, None] <= i[None, :])
    cst[:, 256:384] = np.where(i[None, :] >= i[:, None], 0.0, NEG)
    cst[:, 384:512] = 1.0
    for h in range(16):
        cst[h, 512 + h * 128: 512 + (h + 1) * 128] = 1.0
    for gi in range(4):
        w = 2 << gi
        cst[:, 2560 + gi * 16: 2560 + (gi + 1) * 16] = 1.0 / np.minimum(np.arange(16) + 1, w)
    slopes = _alibi_slopes(12)
    ab = np.zeros((128, 3584), np.float32)
    k = i[:, None].astype(np.float32)
    q = i[None, :].astype(np.float32)
    for hd in range(12):
        g = hd // 4
        a = float(slopes[hd]) * GROUPS[g][1]
        ab[:, hd * 256: hd * 256 + 128] = np.where(k >= q, -a * (q + 128 - k), NEG)
        ab[:, hd * 256 + 128: hd * 256 + 256] = np.where(k <= q, -a * (q - k), NEG)
        if g == 2:
            j = hd - 8
            ab[:, 3072 + j * 128: 3072 + (j + 1) * 128] = np.where(k >= 64, -a * (q + 128 - k), NEG)
    return cst, ab


_CACHE = {}


def _get_program():
    if 'nc' not in _CACHE:
        reqs = build_program(None)
        nc, nops, cnt = build_program(reqs)
        _CACHE['nc'] = nc
    return _CACHE['nc']


def kernel(x, c, w_ada, b_ada, norm1_w, norm2_w, w_in, conv_w, conv_b, dt_bias, a_log, d_skip,
           ssd_norm_w, w_ssd_out, q_norm_w, k_norm_w, w_attn_out, w_pool_mix, pool_scale,
           w_pool_out, w_out, w_ff1, w_ff2):
    f = lambda a: np.asarray(a, np.float32)
    x, c, w_ada, b_ada, norm1_w, norm2_w, w_in = map(f, (x, c, w_ada, b_ada, norm1_w, norm2_w, w_in))
    conv_w, conv_b, dt_bias, a_log, d_skip, ssd_norm_w = map(f, (conv_w, conv_b, dt_bias, a_log, d_skip, ssd_norm_w))
    w_ssd_out, q_norm_w, k_norm_w, w_attn_out, w_pool_mix = map(f, (w_ssd_out, q_norm_w, k_norm_w, w_attn_out, w_pool_mix))
    pool_scale, w_pool_out, w_out, w_ff1, w_ff2 = map(f, (pool_scale, w_pool_out, w_out, w_ff1, w_ff2))

    offs = np.cumsum([0, 1024, 1536, 16, 1536, 1536, 1536, 1024, 3072])
    wst = np.zeros((L * 144, 128, 1024), np.float32)
    wao = np.zeros((L * 8, 128, 512), np.float32)
    wpm = np.zeros((L * 8, 128, 256), np.float32)
    wf2 = np.zeros((L * 16, 128, 2048), np.float32)
    wdt = np.zeros((L, 128, 128), np.float32)
    wada = np.zeros((L * 48, 128, 1024), np.float32)
    vecs = np.zeros((4, 128, L * VL + 8), np.float32)
    rows = np.zeros((128, L * 48), np.float32)
    for l in range(L):
        wi = w_in[l]
        parts = [wi[:, offs[i]:offs[i + 1]] for i in range(8)]
        base = l * 144
        wst[base + 136: base + 144] = _blk_st(parts[0])
        wst[base + 0: base + 12] = _blk_st(parts[1])
        wst[base + 12: base + 24] = _blk_st(parts[3])
        wst[base + 24: base + 36] = _blk_st(parts[4])
        wst[base + 36: base + 48] = _blk_st(parts[5])
        wst[base + 48: base + 56] = _blk_st(parts[6])
        wst[base + 56: base + 80] = _blk_st(parts[7])
        wst[base + 80: base + 88] = _blk_st(w_ssd_out[l])
        wst[base + 88: base + 96] = _blk_st(w_pool_out[l])
        wst[base + 96: base + 104] = _blk_st(w_out[l])
        wst[base + 104: base + 136] = _blk_st(w_ff1[l])
        wao[l * 8:(l + 1) * 8] = _blk_st(w_attn_out[l])
        for gi in range(4):
            wpm[l * 8 + gi * 2: l * 8 + gi * 2 + 2] = _blk_st(w_pool_mix[l, gi])
        f2 = _blk_st(w_ff2[l])
        wf2[l * 16:(l + 1) * 16] = f2.reshape(8, 128, 2, 2048).transpose(0, 2, 1, 3).reshape(16, 128, 2048)
        wdt[l] = _blk_st(parts[2], 16)[0]
        wada[l * 48:(l + 1) * 48] = _blk_st(w_ada[l])
        o = l * VL
        for b in range(4):
            v = vecs[b]
            v[:, o + 0:o + 8] = _pvec(norm1_w[l])
            v[:, o + 8:o + 16] = _pvec(norm2_w[l])
            v[:, o + 16:o + 64] = _pvec(b_ada[l])
            for k in range(4):
                v[:, o + 64 + k * 12: o + 64 + (k + 1) * 12] = _pvec(conv_w[l, k])
            v[:, o + 112:o + 124] = _pvec(conv_b[l])
            v[:, o + 124:o + 132] = _pvec(ssd_norm_w[l])
            v[:, o + 132:o + 140] = _pvec(pool_scale[l])
            v[:, o + 140] = q_norm_w[l]
            v[:, o + 141] = k_norm_w[l]
        rows[:, l * 48: l * 48 + 16] = dt_bias[l][None, :]
        rows[:, l * 48 + 16: l * 48 + 32] = a_log[l][None, :]
        rows[:, l * 48 + 32: l * 48 + 48] = d_skip[l][None, :]
    for b in range(4):
        vecs[b][:, L * VL: L * VL + 8] = _pvec(c[b])
    cst, ab = _consts()
    nc = _get_program()
    shared = dict(wst=wst, wao=wao, wpm=wpm, wf2=wf2, wdt=wdt, wada=wada, rows=rows, cst=cst, abias=ab)
    in_maps = []
    for core in range(N_CORES):
        m = dict(shared)
        m['x'] = np.ascontiguousarray(x[core])
        m['vecs'] = vecs[core]
        in_maps.append(m)
    res = run_bass_kernel_spmd(nc, in_maps, core_ids=list(range(N_CORES)))
    if DUMPS:
        for k_ in res.results[0]:
            if k_.startswith('dbg_'):
                LAST[k_] = np.asarray(res.results[0][k_])
    out = np.stack([np.asarray(res.results[b]['y'], np.float32) for b in range(4)], axis=0)
    return out
```
